# Optimizing a Trainium2 kernel written in Bass

```python
import math
import jax, jax.numpy as jnp
from jax import lax
import numpy as np

D_MODEL = 4096
BATCH = 1
SEQ = 16384
DEPTH = 1

N_META = 16
MIX_WIDTH = D_MODEL
ATTN_WIDTH = MIX_WIDTH // 2
CONV_WIDTH = MIX_WIDTH - ATTN_WIDTH
N_HEADS = 8
HEAD_DIM = ATTN_WIDTH // (2 * N_HEADS)
V_DIM = 2 * HEAD_DIM
QK_COLS = N_HEADS * 2 * HEAD_DIM
V_COLS = N_HEADS * V_DIM
GLU_COLS = 2 * CONV_WIDTH
IN_COLS = 2 * QK_COLS + V_COLS + GLU_COLS
CONV_K = 31
CONV_HALF = CONV_K // 2
D_FF = ((8 * D_MODEL + 3 * 256 - 1) // (3 * 256)) * 256
Q_BLOCK = 128
EPS = 1e-6
LN_EPS = 1e-5

kernel_name = 'hymba_diffattn_conformer_encoder'


def rms_norm(x, g):
    xf = x.astype(jnp.float32)
    y = xf * lax.rsqrt(jnp.mean(xf * xf, axis=-1, keepdims=True) + EPS)
    return (y * g.astype(jnp.float32)).astype(x.dtype)


def layer_norm(x, g, b):
    xf = x.astype(jnp.float32)
    mu = jnp.mean(xf, axis=-1, keepdims=True)
    xc = xf - mu
    y = xc * lax.rsqrt(jnp.mean(xc * xc, axis=-1, keepdims=True) + LN_EPS)
    return (y * g.astype(jnp.float32) + b.astype(jnp.float32)).astype(x.dtype)


def alibi_slopes(n):
    start = 2.0 ** (-8.0 / n)
    return jnp.asarray([start ** (i + 1) for i in range(n)], dtype=jnp.float32)


def diff_attention(q, k, v, lam):
    B, L, H, _, Dk = q.shape
    Dv = v.shape[-1]
    n_blk = -(-L // Q_BLOCK)
    pad = n_blk * Q_BLOCK - L
    qp = jnp.pad(q, ((0, 0), (0, pad), (0, 0), (0, 0), (0, 0)))
    qb = qp.reshape(B, n_blk, Q_BLOCK, H, 2, Dk).transpose(1, 0, 2, 3, 4, 5)
    slopes = alibi_slopes(H)
    kpos = jnp.arange(L, dtype=jnp.int32)
    scale = Dk ** -0.5

    def one_block(args):
        qblk, blk = args
        qpos = blk * Q_BLOCK + jnp.arange(Q_BLOCK, dtype=jnp.int32)
        dist = jnp.abs(qpos[:, None] - kpos[None, :]).astype(jnp.float32)
        bias = -slopes[:, None, None] * dist
        s = jnp.einsum('bqhmd,bkhmd->bhmqk', qblk, k).astype(jnp.float32) * scale
        s = s + bias[None, :, None, :, :]
        p = jax.nn.softmax(s, axis=-1)
        o = jnp.einsum('bhmqk,bkhe->bqhme', p.astype(v.dtype), v).astype(jnp.float32)
        return o[..., 0, :] - lam * o[..., 1, :]

    out = lax.map(one_block, (qb, jnp.arange(n_blk, dtype=jnp.int32)))
    out = out.transpose(1, 0, 2, 3, 4).reshape(B, n_blk * Q_BLOCK, H, Dv)[:, :L]
    return out.astype(v.dtype)


def conformer_conv(u, b_glu, dw_k, dw_b, ln_g, ln_b):
    C = dw_k.shape[-1]
    u = u + b_glu
    a, gate = jnp.split(u, 2, axis=-1)
    h = a * jax.nn.sigmoid(gate)
    h = lax.conv_general_dilated(
        h, dw_k[:, None, :].astype(h.dtype), window_strides=(1,),
        padding=[(CONV_HALF, CONV_HALF)],
        dimension_numbers=('NWC', 'WIO', 'NWC'),
        feature_group_count=C) + dw_b
    h = layer_norm(h, ln_g, ln_b)
    return jax.nn.silu(h)


def setup_inputs(seed: int = 0) -> dict:
    key = jax.random.key(seed)
    ks = jax.random.split(key, 24)
    f32 = jnp.float32
    nrm = lambda k, shape, s: (jax.random.normal(k, shape, f32) * s)
    return {
        'x': nrm(ks[0], (BATCH, SEQ, D_MODEL), 1.0),
        'meta_tokens': nrm(ks[1], (N_META, D_MODEL), 1.0),
        'norm_mix_g': 1.0 + nrm(ks[2], (DEPTH, D_MODEL), 0.02),
        'w_in': nrm(ks[3], (DEPTH, D_MODEL, IN_COLS), D_MODEL ** -0.5),
        'lambda_q1': nrm(ks[4], (DEPTH, HEAD_DIM), 0.1),
        'lambda_k1': nrm(ks[5], (DEPTH, HEAD_DIM), 0.1),
        'lambda_q2': nrm(ks[6], (DEPTH, HEAD_DIM), 0.1),
        'lambda_k2': nrm(ks[7], (DEPTH, HEAD_DIM), 0.1),
        'subln_g': 1.0 + nrm(ks[8], (DEPTH, V_DIM), 0.02),
        'b_glu': nrm(ks[9], (DEPTH, GLU_COLS), 0.02),
        'dw_kernel': nrm(ks[10], (DEPTH, CONV_K, CONV_WIDTH), CONV_K ** -0.5),
        'dw_bias': nrm(ks[11], (DEPTH, CONV_WIDTH), 0.02),
        'conv_ln_g': 1.0 + nrm(ks[12], (DEPTH, CONV_WIDTH), 0.02),
        'conv_ln_b': nrm(ks[13], (DEPTH, CONV_WIDTH), 0.02),
        'w_out': nrm(ks[14], (DEPTH, MIX_WIDTH, D_MODEL), MIX_WIDTH ** -0.5),
        'norm_ffn_g': 1.0 + nrm(ks[15], (DEPTH, D_MODEL), 0.02),
        'w_gate': nrm(ks[16], (DEPTH, D_MODEL, D_FF), D_MODEL ** -0.5),
        'w_up': nrm(ks[17], (DEPTH, D_MODEL, D_FF), D_MODEL ** -0.5),
        'w_down': nrm(ks[18], (DEPTH, D_FF, D_MODEL), D_FF ** -0.5),
        'final_norm_g': 1.0 + nrm(ks[19], (D_MODEL,), 0.02),
    }


def reference(x, meta_tokens, norm_mix_g, w_in, lambda_q1, lambda_k1, lambda_q2, lambda_k2,
              subln_g, b_glu, dw_kernel, dw_bias, conv_ln_g, conv_ln_b, w_out,
              norm_ffn_g, w_gate, w_up, w_down, final_norm_g):
    B = x.shape[0]
    meta = jnp.broadcast_to(meta_tokens.astype(x.dtype)[None], (B, N_META, D_MODEL))
    h = jnp.concatenate([meta, x], axis=1)
    L = h.shape[1]
    for l in range(DEPTH):
        hn = rms_norm(h, norm_mix_g[l])
        proj = hn @ w_in[l]
        q, k, v, u = jnp.split(proj, [QK_COLS, 2 * QK_COLS, 2 * QK_COLS + V_COLS], axis=-1)
        q = q.reshape(B, L, N_HEADS, 2, HEAD_DIM)
        k = k.reshape(B, L, N_HEADS, 2, HEAD_DIM)
        v = v.reshape(B, L, N_HEADS, V_DIM)
        lam_init = 0.8 - 0.6 * math.exp(-0.3 * l)
        lam = (jnp.exp(jnp.sum(lambda_q1[l].astype(jnp.float32) * lambda_k1[l].astype(jnp.float32)))
               - jnp.exp(jnp.sum(lambda_q2[l].astype(jnp.float32) * lambda_k2[l].astype(jnp.float32)))
               + lam_init)
        a = diff_attention(q, k, v, lam)
        a = rms_norm(a, subln_g[l]) * (1.0 - lam_init)
        a = a.reshape(B, L, ATTN_WIDTH)
        c = conformer_conv(u, b_glu[l], dw_kernel[l], dw_bias[l], conv_ln_g[l], conv_ln_b[l])
        h = h + jnp.concatenate([a, c], axis=-1) @ w_out[l]
        hn = rms_norm(h, norm_ffn_g[l])
        h = h + (jax.nn.silu(hn @ w_gate[l]) * (hn @ w_up[l])) @ w_down[l]
    h = rms_norm(h, final_norm_g)
    return h[:, N_META:]
```

```python
import math
import numpy as np
import concourse.bass as bass
import concourse.mybir as mybir
from concourse.bass_utils import run_bass_kernel_spmd

F32, BF16 = mybir.dt.float32, mybir.dt.bfloat16
AF = mybir.ActivationFunctionType
ALU = mybir.AluOpType

NCORES = 8
D = 4096
KC = 32
SEQ = 16384
NMETA = 16
L = SEQ + NMETA
TQ = 2048
NH = 8
DFF = 11008
FC = DFF // 128
EPS = 1e-6
LN_EPS = 1e-5
SCALE = 128 ** -0.5
LAM_INIT = 0.2
SLOPES = [2.0 ** -(i + 1) for i in range(NH)]
WLOC = [512, 1024, 1536, 3072]
WMAX = 3072
NW = TQ + 2 * WMAX
NG = 16896
NGT = 129
NE = TQ + 32
E0 = WMAX - 16
ZTHR = 176.0
SBUF_BASE = 16576
SBUF_LIM = 229344


class Rec:
    __slots__ = ("eng", "fn", "deps", "needed", "signo", "dsem", "dcount", "idx")


class Buf:
    def __init__(self, name):
        self.name = name
        self.w = {}
        self.r = {}
        self.dsem = None


class TL:
    def __init__(self, a, b):
        self.a = a
        self.b = b


class Prog:
    CE = ("act", "dve", "pool", "pe")
    ALLE = ("sync", "act", "dve", "pool", "pe")

    def __init__(self, nc):
        self.nc = nc
        self.recs = {e: [] for e in self.ALLE}
        self.dsem_counts = []
        self.dsem_free = []
        self.phase_bufs = []
        self.off = SBUF_BASE
        self.pers_off = SBUF_BASE
        self.nt = 0

    def tile(self, name, shape, dtype, pers=False):
        nb = int(np.prod(shape[1:])) * (2 if dtype == BF16 else 4)
        nb = (nb + 63) // 64 * 64
        self.nt += 1
        t = self.nc.alloc_sbuf_tensor_at("%s_%d" % (name, self.nt), list(shape), dtype, offset=self.off)
        self.off += nb
        assert self.off <= SBUF_LIM, ("sbuf overflow", name, self.off)
        b = Buf(name)
        if pers:
            self.pers_off = self.off
        else:
            self.phase_bufs.append(b)
        return TL(t.ap(), b)

    def _ev(self, rec):
        if rec.dsem is not None:
            return ("d", rec.dsem), rec.dcount
        return rec.eng, rec

    def _add(self, rec, evmap):
        for k, v in evmap.items():
            if k == "pe" and rec.eng == "pe":
                continue
            if isinstance(k, tuple):
                if rec.deps.get(k, -1) < v:
                    rec.deps[k] = v
            else:
                o = rec.deps.get(k)
                if o is None or o.idx < v.idx:
                    rec.deps[k] = v
                v.needed = True

    def emit(self, eng, fn, reads=(), writes=(), pwrites=(), dma=None):
        rec = Rec()
        rec.eng, rec.fn, rec.deps, rec.needed, rec.signo = eng, fn, {}, False, 0
        rec.dsem, rec.dcount = None, 0
        rec.idx = len(self.recs[eng])
        if dma is not None:
            b = dma.b if isinstance(dma, TL) else dma
            if b.dsem is None:
                if self.dsem_free:
                    b.dsem = self.dsem_free.pop()
                else:
                    b.dsem = len(self.dsem_counts)
                    self.dsem_counts.append(0)
            self.dsem_counts[b.dsem] += 16
            rec.dsem, rec.dcount = b.dsem, self.dsem_counts[b.dsem]
        for t in reads:
            b = t.b if isinstance(t, TL) else t
            self._add(rec, b.w)
        for t in list(writes) + list(pwrites):
            b = t.b if isinstance(t, TL) else t
            self._add(rec, b.r)
        for t in writes:
            b = t.b if isinstance(t, TL) else t
            self._add(rec, b.w)
        k, v = self._ev(rec)
        for t in reads:
            b = t.b if isinstance(t, TL) else t
            b.r[k] = v
        for t in writes:
            b = t.b if isinstance(t, TL) else t
            b.w = {k: v}
            b.r = {}
        for t in pwrites:
            b = t.b if isinstance(t, TL) else t
            if b.r:
                b.w = {}
                b.r = {}
            b.w[k] = v
        self.recs[eng].append(rec)
        return rec

    def barrier(self, keep=None):
        lasts = {}
        for e in self.CE:
            if self.recs[e]:
                r = None
                for x in reversed(self.recs[e]):
                    if x.fn is not None and x.dsem is None:
                        r = x
                        break
                if r is not None:
                    lasts[e] = r
        for e in self.ALLE:
            rec = Rec()
            rec.eng, rec.fn, rec.deps, rec.needed, rec.signo = e, None, {}, False, 0
            rec.dsem, rec.dcount = None, 0
            rec.idx = len(self.recs[e])
            for ce, r in lasts.items():
                if ce == e and e == "pe":
                    continue
                rec.deps[ce] = r
                r.needed = True
            for i, c in enumerate(self.dsem_counts):
                if c > 0:
                    rec.deps[("d", i)] = c
            self.recs[e].append(rec)
        for b in self.phase_bufs:
            if b.dsem is not None:
                self.dsem_free.append(b.dsem)
                b.dsem = None
        self.phase_bufs = []
        self.off = self.pers_off if keep is None else keep

    def finish(self):
        nc = self.nc
        for e in self.CE:
            n = 0
            for r in self.recs[e]:
                if r.needed:
                    n += 1
                    r.signo = n
        nd = len(self.dsem_counts)
        csem = {e: nc.alloc_semaphore("cs_" + e) for e in self.CE}
        dsem = [nc.alloc_semaphore("ds_%d" % i) for i in range(nd)]
        recs = self.recs

        def run(E, ename):
            waited = {}
            for r in recs[ename]:
                for k, v in r.deps.items():
                    if isinstance(k, tuple):
                        sem, val = dsem[k[1]], v
                    else:
                        sem, val = csem[k], v.signo
                    if waited.get(k, -1) >= val:
                        continue
                    waited[k] = val
                    E.wait_ge(sem, val)
                if r.fn is None:
                    continue
                ins = r.fn(E)
                if r.needed:
                    ins.then_inc(csem[ename], 1)
                if r.dsem is not None:
                    ins.then_inc(dsem[r.dsem], 16)

        with nc.Block() as block:
            @block.sync
            def _(E):
                run(E, "sync")

            @block.scalar
            def _(E):
                run(E, "act")

            @block.vector
            def _(E):
                run(E, "dve")

            @block.gpsimd
            def _(E):
                run(E, "pool")

            @block.tensor
            def _(E):
                run(E, "pe")


def local_entries():
    ents, col = {}, 0
    for h in range(4):
        sl = SLOPES[h]
        for j in range(4):
            lst = []
            for t in range(4 * j, 4 * j + (512 + 2 * WLOC[h]) // 128):
                dl = 512 * j + WLOC[h] - 128 * t
                if (dl >= 128 and sl * (dl - 127) > ZTHR) or (dl <= -512 and sl * (-dl - 511) > ZTHR):
                    col += 1
                    continue
                if dl >= 128:
                    lst.append((t, "f", -sl, -sl * dl, col))
                elif dl <= -512:
                    lst.append((t, "f", sl, sl * dl, col))
                else:
                    lst.append((t, "d", (-dl) // 128, 0.0, col))
                col += 1
            ents[(h, j)] = lst
    return ents, col


def build_program(stop_after=None):
    nc = bass.Bass("TRN2", target_bir_lowering=False)
    P = Prog(nc)

    def din(name, shape, dt=F32):
        return nc.dram_tensor(name, list(shape), dt, kind="ExternalInput").ap()

    xw = din("xw", [NW, D])
    xg = din("xg", [NG, D])
    w_in = din("w_in", [D, 10240])
    w_out = din("w_out", [D, D])
    w_gate = din("w_gate", [D, DFF])
    w_up = din("w_up", [D, DFF])
    w_down = din("w_down", [DFF, D])
    g_mix = din("g_mix", [D])
    g_ffn = din("g_ffn", [D])
    g_fin = din("g_fin", [D])
    g_sub = din("g_sub", [256])
    cst = din("cst", [128, 1024])
    ptab = din("ptab", [128, 384])
    dwk_d = din("dwk", [128, 16 * 31])
    adtab = din("adtab", [128, 4, 512])
    gtab = din("gtab", [4, 128, 1032])
    ltab = din("ltab", [128, local_entries()[1]])
    out_d = nc.dram_tensor("out", [TQ, D], F32, kind="ExternalOutput").ap()

    hwT = nc.dram_tensor("hwT", [NW // 512, 128, KC, 512], BF16).ap()
    hgT = nc.dram_tensor("hgT", [NG // 512, 128, KC, 512], BF16).ap()
    spans = [TQ + 2 * w for w in WLOC] + [NG] * 4
    kT_d = [nc.dram_tensor("kT%d" % h, [2, 128, spans[h]], BF16).ap() for h in range(NH)]
    v_d = [nc.dram_tensor("v%d" % h, [128, spans[h] // 128, 257], BF16).ap() for h in range(NH)]
    qT_d = nc.dram_tensor("qT", [16, 128, TQ], BF16).ap()
    hglu_d = nc.dram_tensor("hglu", [16, 128, NE], BF16).ap()
    mixT_d = nc.dram_tensor("mixT", [KC, 128, TQ], BF16).ap()
    h1_d = nc.dram_tensor("h1", [TQ, D], F32).ap()
    h2_d = nc.dram_tensor("h2", [TQ, D], F32).ap()
    hn2T_d = nc.dram_tensor("hn2T", [TQ // 512, 128, KC, 512], BF16).ap()
    B_hwT = [Buf("hwT%d" % i) for i in range(NW // 512)]
    B_hgT = [Buf("hgT%d" % i) for i in range(NG // 512)]
    B_kT = [Buf("kT%d" % h) for h in range(NH)]
    B_v = [Buf("v%d" % h) for h in range(NH)]
    B_qT = Buf("qT")
    B_hglu = Buf("hglu")
    B_mixT = Buf("mixT")
    B_h1 = Buf("h1")
    B_h2 = Buf("h2")
    B_hn2T = [Buf("hn2T%d" % i) for i in range(4)]
    B_out = Buf("out")

    PS = []
    for i in range(8):
        t = nc.alloc_psum_tensor("psb%d" % i, [128, 512], F32)
        PS.append(TL(t.ap(), Buf("ps%d" % i)))

    cst_t = P.tile("cst", [128, 1024], F32, pers=True)
    ptab_t = P.tile("ptab", [128, 384], F32, pers=True)
    dwk_t = P.tile("dwk", [128, 16 * 31], F32, pers=True)
    ident = P.tile("ident", [128, 128], BF16, pers=True)
    gsub_t = P.tile("gsub", [128, 256], F32, pers=True)
    lam_t = P.tile("lamt", [128, 8], F32, pers=True)
    P.emit("sync", lambda E: E.dma_start(out=cst_t.a[:], in_=cst[:, :]), writes=[cst_t], dma=cst_t)
    P.emit("sync", lambda E: E.dma_start(out=ptab_t.a[:], in_=ptab[:, :]), writes=[ptab_t], dma=ptab_t)
    P.emit("sync", lambda E: E.dma_start(out=dwk_t.a[:], in_=dwk_d[:, :]), writes=[dwk_t], dma=dwk_t)
    P.emit("sync", lambda E: E.dma_start(out=gsub_t.a[:], in_=g_sub.partition_broadcast(128)), writes=[gsub_t], dma=gsub_t)
    P.emit("dve", lambda E: E.tensor_copy(out=ident.a[:], in_=cst_t.a[:, 0:128]), reads=[cst_t], writes=[ident])
    ones_f = cst_t.a[:, 128:256]
    D0 = cst_t.a[:, 256:768]
    PT_BGLU, PT_DWB, PT_LNG, PT_LNB, PT_MW, PT_MG, PT_EM = 0, 32, 48, 64, 80, 80 + NW // 128, 80 + NW // 128 + NG // 128
    pt = ptab_t.a
    P.emit("dve", lambda E: E.tensor_tensor(out=lam_t.a[:, 0:1], in0=cst_t.a[:, 768:769], in1=cst_t.a[:, 769:770], op=ALU.mult),
           reads=[cst_t], pwrites=[lam_t])
    P.emit("dve", lambda E: E.tensor_tensor(out=lam_t.a[:, 1:2], in0=cst_t.a[:, 770:771], in1=cst_t.a[:, 771:772], op=ALU.mult),
           reads=[cst_t], pwrites=[lam_t])
    P.emit("pe", lambda E: E.matmul(PS[0].a[:, 0:2], lhsT=ones_f, rhs=lam_t.a[:, 0:2], start=True, stop=True),
           reads=[cst_t, lam_t], writes=[PS[0]])
    P.emit("act", lambda E: E.activation(out=lam_t.a[:, 2:4], in_=PS[0].a[:, 0:2], func=AF.Exp), reads=[PS[0]], pwrites=[lam_t])
    P.emit("dve", lambda E: E.tensor_tensor(out=lam_t.a[:, 4:5], in0=lam_t.a[:, 2:3], in1=lam_t.a[:, 3:4], op=ALU.subtract),
           reads=[lam_t], pwrites=[lam_t])
    P.emit("dve", lambda E: E.tensor_scalar(out=lam_t.a[:, 5:6], in0=lam_t.a[:, 4:5], scalar1=LAM_INIT, scalar2=-1.0,
                                            op0=ALU.add, op1=ALU.mult), reads=[lam_t], pwrites=[lam_t])
    NEGLAM = lam_t.a[:, 5:6]
    LN_SUBSCALE = cst_t.a[:, 772:773]
    EPS_COL = cst_t.a[:, 773:774]
    LNEPS_COL = cst_t.a[:, 774:775]

    def norm_T(srcs, g_dram, src_buf=None):
        tl = [(a, i, d, b) for (a, n, d, b) in srcs for i in range(n)]
        ntiles = len(tl)
        gb = P.tile("gb", [128, D], F32)
        P.emit("sync", lambda E: E.dma_start(out=gb.a[:], in_=g_dram.partition_broadcast(128)), writes=[gb], dma=gb)
        xt = [P.tile("xt", [128, D], F32) for _ in range(3)]
        sq = P.tile("sq", [128, D], BF16)
        xs = [P.tile("xs", [128, D], BF16) for _ in range(2)]
        st = [P.tile("st", [128, 2], F32) for _ in range(2)]
        hst = [P.tile("hst", [128, KC, 512], BF16) for _ in range(2)]
        psb = PS[0:4]
        def stage0(i):
            x_ = xt[i % 3]
            rd = [src_buf] if src_buf is not None else []
            src, li = tl[i][0], tl[i][1]
            P.emit("sync", lambda E, x_=x_, li=li, src=src: E.dma_start(out=x_.a[:], in_=src[li * 128:(li + 1) * 128, :]),
                   reads=rd, writes=[x_], dma=x_)

        def stage1(i):
            x_, s_, xs_ = xt[i % 3], st[i % 2], xs[i % 2]
            P.emit("act", lambda E, x_=x_, s_=s_: E.activation(out=sq.a[:], in_=x_.a[:], func=AF.Square, accum_out=s_.a[:, 0:1]),
                   reads=[x_], writes=[sq], pwrites=[s_])
            P.emit("act", lambda E, s_=s_: E.activation(out=s_.a[:, 1:2], in_=s_.a[:, 0:1], func=AF.Ln, scale=1.0 / D, bias=EPS_COL),
                   reads=[s_, cst_t], pwrites=[s_])
            P.emit("act", lambda E, s_=s_: E.activation(out=s_.a[:, 1:2], in_=s_.a[:, 1:2], func=AF.Exp, scale=-0.5),
                   reads=[s_], pwrites=[s_])
            P.emit("dve", lambda E, x_=x_, s_=s_, xs_=xs_: E.scalar_tensor_tensor(
                out=xs_.a[:], in0=x_.a[:], scalar=s_.a[:, 1:2], in1=gb.a[:], op0=ALU.mult, op1=ALU.mult),
                reads=[x_, s_, gb], writes=[xs_])

        def stage2(i):
            xs_ = xs[i % 2]
            hs = hst[(i // 4) % 2]
            li, dstT, dst_bufs = tl[i][1], tl[i][2], tl[i][3]
            for g in range(4):
                pb = psb[g]
                pbv = pb.a.bitcast(BF16)
                for k in range(8):
                    kc = g * 8 + k
                    P.emit("pe", lambda E, pbv=pbv, k=k, kc=kc, xs_=xs_: E.transpose(
                        out=pbv[:, k * 128:(k + 1) * 128], in_=xs_.a[:, kc * 128:(kc + 1) * 128], identity=ident.a[:]),
                        reads=[xs_, ident], writes=[pb] if k == 0 else [], pwrites=[] if k == 0 else [pb])
                c0 = (li % 4) * 128
                if g < 2:
                    P.emit("act", lambda E, pbv=pbv, hs=hs, g=g, c0=c0: E.activation(
                        out=hs.a[:, g * 8:(g + 1) * 8, c0:c0 + 128], in_=pbv.rearrange("p (k t) -> p k t", k=8), func=AF.Copy),
                        reads=[pb], pwrites=[hs])
                else:
                    P.emit("dve", lambda E, pbv=pbv, hs=hs, g=g, c0=c0: E.tensor_copy(
                        out=hs.a[:, g * 8:(g + 1) * 8, c0:c0 + 128], in_=pbv.rearrange("p (k t) -> p k t", k=8)),
                        reads=[pb], pwrites=[hs])
            if li % 4 == 3:
                blk = li // 4
                P.emit("sync", lambda E, hs=hs, blk=blk, dstT=dstT: E.dma_start(out=dstT[blk], in_=hs.a[:]),
                       reads=[hs], pwrites=[dst_bufs[blk]], dma=hs)

        assert all(n % 4 == 0 for (_, n, _, _) in srcs)
        stage0(0)
        stage0(1)
        for i in range(ntiles + 1):
            if i + 2 < ntiles:
                stage0(i + 2)
            if i < ntiles:
                stage1(i)
            if i >= 1:
                stage2(i - 1)
        P.barrier()

    norm_T([(xw, NW // 128, hwT, B_hwT), (xg, NG // 128, hgT, B_hgT)], g_mix)
    if stop_after == "A":
        P.finish()
        return nc

    def wload(dst, wsrc, c0, ncols, r0=0, nk=KC):
        P.emit("pool", lambda E: E.dma_start(
            out=dst.a[:, 0:nk, 0:ncols],
            in_=wsrc[r0:r0 + nk * 128, c0:c0 + ncols].rearrange("(k p) c -> p k c", p=128)),
            writes=[dst], dma=dst)

    wk = [P.tile("wk", [128, KC, 256], BF16) for _ in range(2)]
    wv = [P.tile("wv", [128, KC, 256], BF16) for _ in range(2)]
    hb = [P.tile("hb", [128, KC, 512], BF16) for _ in range(3)]
    kst = [P.tile("kst", [128, 2, 512], BF16) for _ in range(2)]
    vst = [P.tile("vst", [128, 4, 257], BF16) for _ in range(2)]
    blocks = []
    for h in range(NH):
        local = h < 4
        for b in range(spans[h] // 512):
            if local or b >= NG // 512:
                col0 = (WMAX - WLOC[h] + 512 * b) if local else (WMAX + 512 * (b - NG // 512))
                blocks.append((h, b, hwT, [B_hwT[col0 // 512]], col0, PT_MW + col0 // 128))
            else:
                col0 = 512 * b
                blocks.append((h, b, hgT, [B_hgT[b]], col0, PT_MG + col0 // 128))

    def bload(i):
        h, b, srcT, sb_, col0, mcol = blocks[i]
        hb_ = hb[i % 3]
        assert col0 % 512 == 0
        P.emit("sync", lambda E: E.dma_start(out=hb_.a[:], in_=srcT[col0 // 512]), reads=sb_, writes=[hb_], dma=hb_)

    bload(0)
    bload(1)
    for it in range(len(blocks)):
        h, b, srcT, sb_, col0, mcol = blocks[it]
        wk_, wv_ = wk[h % 2], wv[h % 2]
        if b == 0:
            wload(wk_, w_in, 2048 + 256 * h, 256)
            wload(wv_, w_in, 4096 + 256 * h, 256)
        if it + 2 < len(blocks):
            bload(it + 2)
        if True:
            hb_, ks_, vs_ = hb[it % 3], kst[it % 2], vst[it % 2]
            for m in range(2):
                pb = PS[(it * 2 + m) % 4]
                for kc in range(KC):
                    P.emit("pe", lambda E, pb=pb, wk_=wk_, hb_=hb_, m=m, kc=kc: E.matmul(
                        pb.a[:], lhsT=wk_.a[:, kc, m * 128:(m + 1) * 128], rhs=hb_.a[:, kc, :],
                        start=(kc == 0), stop=(kc == KC - 1)),
                        reads=[wk_, hb_], writes=[pb] if kc == 0 else [], pwrites=[] if kc == 0 else [pb])
                P.emit("act", lambda E, pb=pb, ks_=ks_, m=m: E.activation(out=ks_.a[:, m, :], in_=pb.a[:], func=AF.Copy),
                       reads=[pb], pwrites=[ks_])
            P.emit("sync", lambda E, ks_=ks_, h=h, b=b: E.dma_start(
                out=kT_d[h][:, :, b * 512:(b + 1) * 512].rearrange("m p t -> p m t"), in_=ks_.a[:]),
                reads=[ks_], pwrites=[B_kT[h]], dma=ks_)
            for s in range(4):
                pb = PS[4 + (it * 4 + s) % 4]
                for kc in range(KC):
                    P.emit("pe", lambda E, pb=pb, wv_=wv_, hb_=hb_, s=s, kc=kc: E.matmul(
                        pb.a[:, 0:256], lhsT=hb_.a[:, kc, s * 128:(s + 1) * 128], rhs=wv_.a[:, kc, :],
                        start=(kc == 0), stop=(kc == KC - 1)),
                        reads=[wv_, hb_], writes=[pb] if kc == 0 else [], pwrites=[] if kc == 0 else [pb])
                P.emit("dve", lambda E, pb=pb, vs_=vs_, s=s: E.tensor_copy(out=vs_.a[:, s, 0:256], in_=pb.a[:, 0:256]),
                       reads=[pb], pwrites=[vs_])
                P.emit("dve", lambda E, vs_=vs_, s=s, mc=mcol + s: E.tensor_copy(out=vs_.a[:, s, 256:257], in_=pt[:, mc:mc + 1]),
                       reads=[ptab_t], pwrites=[vs_])
            P.emit("sync", lambda E, vs_=vs_, h=h, b=b: E.dma_start(
                out=v_d[h][:, 4 * b:4 * b + 4, :], in_=vs_.a[:]),
                reads=[vs_], pwrites=[B_v[h]], dma=vs_)
    P.barrier()

    HE = NE // 2
    hq = P.tile("hq", [128, KC, HE], BF16)
    wq = [P.tile("wq", [128, KC, 512], BF16) for _ in range(2)]
    qst = [P.tile("qst", [128, 512], BF16) for _ in range(2)]
    ua = P.tile("ua", [128, 4, HE], F32)
    sg = [P.tile("sg", [128, 416], F32) for _ in range(2)]
    hgs = [P.tile("hgs", [128, HE], BF16) for _ in range(2)]
    UT = [(0, 416), (416, 416), (832, 208)]
    wi = 0
    for half in range(2):
        e0 = half * HE
        c_lo = E0 + e0
        pos = 0
        first = True
        while pos < HE:
            bk, bo = (c_lo + pos) // 512, (c_lo + pos) % 512
            n_ = min(512 - bo, HE - pos)
            P.emit("sync", lambda E, pos=pos, bk=bk, bo=bo, n_=n_: E.dma_start(
                out=hq.a[:, :, pos:pos + n_], in_=hwT[bk][:, :, bo:bo + n_]),
                reads=[B_hwT[bk]], writes=[hq] if first else [], pwrites=[] if first else [hq], dma=hq)
            first = False
            pos += n_
        qoff = 16 if half == 0 else 0
        for g in range(4):
            w_ = wq[wi % 2]
            wi += 1
            wload(w_, w_in, 512 * g, 512)
            for c in range(4):
                for nb in range(2):
                    pb = PS[(c * 2 + nb) % 8]
                    for kc in range(KC):
                        P.emit("pe", lambda E, pb=pb, w_=w_, c=c, nb=nb, kc=kc, qoff=qoff: E.matmul(
                            pb.a[:], lhsT=w_.a[:, kc, c * 128:(c + 1) * 128],
                            rhs=hq.a[:, kc, qoff + nb * 512:qoff + (nb + 1) * 512], start=(kc == 0), stop=(kc == KC - 1)),
                            reads=[w_, hq], writes=[pb] if kc == 0 else [], pwrites=[] if kc == 0 else [pb])
                    q_ = qst[(c * 2 + nb) % 2]
                    P.emit("act", lambda E, pb=pb, q_=q_: E.activation(out=q_.a[:], in_=pb.a[:], func=AF.Copy, scale=SCALE),
                           reads=[pb], writes=[q_])
                    t0 = half * 1024 + nb * 512
                    P.emit("sync", lambda E, q_=q_, cc=g * 4 + c, t0=t0: E.dma_start(out=qT_d[cc, :, t0:t0 + 512], in_=q_.a[:]),
                           reads=[q_], pwrites=[B_qT], dma=q_)
        for j in range(4):
            for part in range(2):
                w_ = wq[wi % 2]
                wi += 1
                wload(w_, w_in, 6144 + 2048 * part + 512 * j, 512)
                for c in range(4):
                    ch = 4 * j + c
                    hg_ = hgs[ch % 2]
                    for ti, (n0, nn) in enumerate(UT):
                        pb = PS[(c * 3 + ti) % 8]
                        for kc in range(KC):
                            P.emit("pe", lambda E, pb=pb, w_=w_, c=c, kc=kc, n0=n0, nn=nn: E.matmul(
                                pb.a[:, 0:nn], lhsT=w_.a[:, kc, c * 128:(c + 1) * 128], rhs=hq.a[:, kc, n0:n0 + nn],
                                start=(kc == 0), stop=(kc == KC - 1)),
                                reads=[w_, hq], writes=[pb] if kc == 0 else [], pwrites=[] if kc == 0 else [pb])
                        if part == 0:
                            P.emit("act", lambda E, pb=pb, c=c, n0=n0, nn=nn, ch=ch: E.activation(
                                out=ua.a[:, c, n0:n0 + nn], in_=pb.a[:, 0:nn], func=AF.Identity,
                                bias=pt[:, PT_BGLU + ch:PT_BGLU + ch + 1]), reads=[pb, ptab_t], pwrites=[ua])
                        else:
                            s_ = sg[ti % 2]
                            P.emit("act", lambda E, pb=pb, s_=s_, nn=nn, ch=ch: E.activation(
                                out=s_.a[:, 0:nn], in_=pb.a[:, 0:nn], func=AF.Sigmoid,
                                bias=pt[:, PT_BGLU + 16 + ch:PT_BGLU + 16 + ch + 1]), reads=[pb, ptab_t], writes=[s_])
                            P.emit("dve", lambda E, s_=s_, c=c, n0=n0, nn=nn, hg_=hg_: E.tensor_tensor(
                                out=hg_.a[:, n0:n0 + nn], in0=ua.a[:, c, n0:n0 + nn], in1=s_.a[:, 0:nn], op=ALU.mult),
                                reads=[s_, ua], pwrites=[hg_])
                    if part == 1:
                        hc = 0 if half == 0 else HE - 16
                        mc = PT_EM + 16 * half
                        P.emit("dve", lambda E, hg_=hg_, hc=hc, mc=mc: E.tensor_tensor(
                            out=hg_.a[:, hc:hc + 16], in0=hg_.a[:, hc:hc + 16], in1=pt[:, mc:mc + 16], op=ALU.mult),
                            reads=[hg_, ptab_t], pwrites=[hg_])
                        P.emit("sync", lambda E, hg_=hg_, ch=ch, e0=e0: E.dma_start(out=hglu_d[ch, :, e0:e0 + HE], in_=hg_.a[:]),
                               reads=[hg_], pwrites=[B_hglu], dma=hg_)
    P.barrier()
    if stop_after == "C":
        P.finish()
        return nc

    hin = [P.tile("hin", [128, 16, 544], BF16) for _ in range(2)]
    dg = [P.tile("dg", [128, 31, 128], BF16) for _ in range(2)]
    yc2 = [[P.tile("y", [128, 512], F32) for _ in range(16)] for _ in range(2)]
    ysq = [P.tile("ysq", [128, 512], F32) for _ in range(2)]
    mst = P.tile("mst", [128, 4, 512], F32)
    t1 = [P.tile("t1", [128, 512], F32) for _ in range(2)]
    cst_ = [P.tile("cstg", [128, 16, 512], BF16) for _ in range(2)]
    dwk3 = dwk_t.a.rearrange("p (c j) -> p c j", j=31)

    def dload(blk):
        hi = hin[blk % 2]
        P.emit("sync", lambda E: E.dma_start(
            out=hi.a[:, :, 0:542], in_=hglu_d[:, :, 512 * blk + 1:512 * blk + 543].rearrange("c p t -> p c t")),
            reads=[B_hglu], writes=[hi], dma=hi)

    dload(0)
    for blk in range(4):
        hi = hin[blk % 2]
        yc = yc2[blk % 2]
        if blk + 1 < 4:
            dload(blk + 1)
        for ch in range(16):
            dg_ = dg[ch % 2]
            for j in range(31):
                P.emit("dve", lambda E, dg_=dg_, ch=ch, j=j: E.tensor_scalar(
                    out=dg_.a[:, j, :], in0=ident.a[:], scalar1=dwk3[:, ch, j:j + 1], scalar2=None, op0=ALU.mult),
                    reads=[ident, dwk_t], pwrites=[dg_])
            pb = PS[2 + ch % 4]
            for j in range(31):
                P.emit("pe", lambda E, pb=pb, dg_=dg_, hi=hi, ch=ch, j=j: E.matmul(
                    pb.a[:], lhsT=dg_.a[:, j, :], rhs=hi.a[:, ch, j:j + 512], start=(j == 0), stop=(j == 30)),
                    reads=[dg_, hi], writes=[pb] if j == 0 else [], pwrites=[] if j == 0 else [pb])
            P.emit("act", lambda E, pb=pb, ch=ch, yc=yc: E.activation(
                out=yc[ch].a[:], in_=pb.a[:], func=AF.Identity, bias=pt[:, PT_DWB + ch:PT_DWB + ch + 1]),
                reads=[pb, ptab_t], writes=[yc[ch]])
            P.emit("pe", lambda E, ch=ch, yc=yc: E.matmul(PS[0].a[:], lhsT=ones_f, rhs=yc[ch].a[:], start=(ch == 0), stop=(ch == 15)),
                   reads=[yc[ch], cst_t], writes=[PS[0]] if ch == 0 else [], pwrites=[] if ch == 0 else [PS[0]])
            q_ = ysq[ch % 2]
            P.emit("act", lambda E, ch=ch, q_=q_, yc=yc: E.activation(out=q_.a[:], in_=yc[ch].a[:], func=AF.Square),
                   reads=[yc[ch]], writes=[q_])
            P.emit("pe", lambda E, ch=ch, q_=q_: E.matmul(PS[1].a[:], lhsT=ones_f, rhs=q_.a[:], start=(ch == 0), stop=(ch == 15)),
                   reads=[q_, cst_t], writes=[PS[1]] if ch == 0 else [], pwrites=[] if ch == 0 else [PS[1]])
        m_ = mst.a
        P.emit("dve", lambda E: E.tensor_scalar(out=m_[:, 0, :], in0=PS[0].a[:], scalar1=1.0 / 2048, scalar2=None, op0=ALU.mult),
               reads=[PS[0]], pwrites=[mst])
        P.emit("dve", lambda E: E.tensor_tensor(out=m_[:, 2, :], in0=m_[:, 0, :], in1=m_[:, 0, :], op=ALU.mult),
               reads=[mst], pwrites=[mst])
        P.emit("dve", lambda E: E.scalar_tensor_tensor(out=m_[:, 1, :], in0=PS[1].a[:], scalar=1.0 / 2048, in1=m_[:, 2, :],
                                                       op0=ALU.mult, op1=ALU.subtract), reads=[PS[1], mst], pwrites=[mst])
        P.emit("act", lambda E: E.activation(out=m_[:, 1, :], in_=m_[:, 1, :], func=AF.Ln, bias=LNEPS_COL),
               reads=[mst, cst_t], pwrites=[mst])
        P.emit("act", lambda E: E.activation(out=m_[:, 1, :], in_=m_[:, 1, :], func=AF.Exp, scale=-0.5), reads=[mst], pwrites=[mst])
        P.emit("dve", lambda E: E.scalar_tensor_tensor(out=m_[:, 2, :], in0=m_[:, 0, :], scalar=-1.0, in1=m_[:, 1, :],
                                                       op0=ALU.mult, op1=ALU.mult), reads=[mst], pwrites=[mst])
        cs = cst_[blk % 2]
        for ch in range(16):
            tt = t1[ch % 2]
            P.emit("dve", lambda E, ch=ch, tt=tt, yc=yc: E.tensor_tensor(out=tt.a[:], in0=yc[ch].a[:], in1=m_[:, 1, :], op=ALU.mult),
                   reads=[yc[ch], mst], writes=[tt])
            P.emit("dve", lambda E, tt=tt: E.tensor_tensor(out=tt.a[:], in0=tt.a[:], in1=m_[:, 2, :], op=ALU.add),
                   reads=[tt, mst], writes=[tt])
            P.emit("act", lambda E, ch=ch, tt=tt, cs=cs: E.activation(
                out=cs.a[:, ch, :], in_=tt.a[:], func=AF.Silu, scale=pt[:, PT_LNG + ch:PT_LNG + ch + 1],
                bias=pt[:, PT_LNB + ch:PT_LNB + ch + 1]), reads=[tt, ptab_t], pwrites=[cs])
        P.emit("sync", lambda E, cs=cs, blk=blk: E.dma_start(
            out=mixT_d[16:32, :, 512 * blk:512 * (blk + 1)].rearrange("c p t -> p c t"), in_=cs.a[:]),
            reads=[cs], pwrites=[B_mixT], dma=cs)
    P.barrier()
    if stop_after == "D":
        P.finish()
        return nc

    NKT = NGT
    LA = 3
    DEFER = 8
    kTs = P.tile("kTs", [128, 2, NKT * 128], BF16)
    vs = P.tile("vs", [128, NKT, 257], BF16)
    qs = P.tile("qs", [128, 2, TQ], BF16)
    gt = P.tile("gt", [128, 1032], F32)
    adt = P.tile("adt", [128, 4, 512], F32)
    LENT, nlcol = local_entries()
    lt = P.tile("lt", [128, nlcol], F32)
    P.emit("sync", lambda E: E.dma_start(out=lt.a[:], in_=ltab[:, :]), writes=[lt], dma=lt)
    sbb = [P.tile("sbb", [128, 512], F32) for _ in range(4)]
    pTb = [P.tile("pTb", [128, 512], BF16) for _ in range(4)]
    o1 = P.tile("o1", [128, 4, 257], F32)
    o2 = P.tile("o2", [128, 4, 257], F32)
    Ft = [P.tile("Ft", [128, 16], F32) for _ in range(2)]
    av = [P.tile("av", [128, 256], F32) for _ in range(4)]
    an = [P.tile("an", [128, 256], BF16) for _ in range(4)]
    sqj = P.tile("sqj", [128, 256], BF16)
    ast = [P.tile("ast", [128, 2, 512], BF16) for _ in range(2)]
    P.emit("sync", lambda E: E.dma_start(out=adt.a[:], in_=adtab[:, :, :]), writes=[adt], dma=adt)
    gctr = [0, 0]
    CHK = [(0, 37), (37, 74), (74, 111), (111, NKT)]
    Bk = [Buf("kch%d" % c) for c in range(4)]
    Bv = [Buf("vch%d" % c) for c in range(4)]
    P.phase_bufs.extend(Bk + Bv)

    def chunk_of(t):
        return min(t // 37, 3)

    def kv_load(h, c):
        lo, hi = CHK[c]
        if h < 4:
            segs = [(lo, min(hi, spans[h] // 128), lo)]
        else:
            segs = [(lo, min(hi, NGT), lo), (max(lo, NGT), hi, NG // 128 + max(lo, NGT) - NGT)]
        first = True
        for (a, b_, s0) in segs:
            if b_ <= a:
                continue
            P.emit("sync", lambda E, h=h, a=a, b_=b_, s0=s0: E.dma_start(
                out=kTs.a[:, :, a * 128:b_ * 128], in_=kT_d[h][:, :, s0 * 128:(s0 + b_ - a) * 128].rearrange("m p t -> p m t")),
                reads=[B_kT[h]], writes=[Bk[c]] if first else [], pwrites=[] if first else [Bk[c]], dma=Bk[c])
            P.emit("sync", lambda E, h=h, a=a, b_=b_, s0=s0: E.dma_start(
                out=vs.a[:, a:b_, :], in_=v_d[h][:, s0:s0 + b_ - a, :]),
                reads=[B_v[h]], writes=[Bv[c]] if first else [], pwrites=[] if first else [Bv[c]], dma=Bv[c])
            first = False

    for c in range(4):
        kv_load(0, c)
    for h in range(NH):
        local = h < 4
        sl = SLOPES[h]
        if not local:
            P.emit("sync", lambda E, h=h: E.dma_start(out=gt.a[:], in_=gtab[h - 4, :, :]), writes=[gt], dma=gt)
        P.emit("sync", lambda E, h=h: E.dma_start(out=qs.a[:], in_=qT_d[2 * h:2 * h + 2, :, :].rearrange("m p t -> p m t")),
               reads=[B_qT], writes=[qs], dma=qs)
        items = []
        for j in range(4):
            ents = []
            if local:
                for (t, mode, ca, cb, col) in LENT[(h, j)]:
                    ents.append((t, mode, ca, lt.a[:, col:col + 1]))
            else:
                for t in range(NGT):
                    c = t * 4 + j
                    if 4 * j <= t < 4 * j + 4:
                        ents.append((t, "d", t - 4 * j, 0.0))
                    elif (16 <= t < 128 and sl * ((t - 4 * j - 4) * 128 + 1) > ZTHR
                          and sl * ((128 - t + 4 * j - 1) * 128 + 1) > ZTHR):
                        continue
                    else:
                        ents.append((t, "f", gt.a[:, c:c + 1], gt.a[:, 516 + c:516 + c + 1]))
            for m in range(2):
                for ti, (t, mode, ca, cb) in enumerate(ents):
                    items.append(dict(j=j, m=m, ti=ti, n=len(ents), t=t, mode=mode, ca=ca, cb=cb))

        def stage1(it_):
            g, g2 = gctr[0], gctr[1]
            gctr[0] += 1
            gctr[1] += 1
            ps_s, sb_, pT_ = PS[4 + g2 % 4], sbb[g % 4], pTb[g % 4]
            it_["pT"] = pT_
            t, m, j, mode, ca, cb = it_["t"], it_["m"], it_["j"], it_["mode"], it_["ca"], it_["cb"]
            nsl = -sl
            P.emit("pe", lambda E: E.matmul(
                ps_s.a[:], lhsT=kTs.a[:, m, t * 128:(t + 1) * 128], rhs=qs.a[:, m, j * 512:(j + 1) * 512],
                start=True, stop=True), reads=[Bk[chunk_of(t)], qs], writes=[ps_s])
            if mode == "d":
                P.emit("dve", lambda E: E.scalar_tensor_tensor(
                    out=sb_.a[:], in0=adt.a[:, ca, :], scalar=nsl, in1=ps_s.a[:], op0=ALU.mult, op1=ALU.add),
                    reads=[adt, ps_s], writes=[sb_])
                P.emit("act", lambda E: E.activation(out=pT_.a[:], in_=sb_.a[:], func=AF.Exp), reads=[sb_], writes=[pT_])
            else:
                P.emit("dve", lambda E: E.scalar_tensor_tensor(
                    out=sb_.a[:], in0=D0, scalar=ca, in1=ps_s.a[:], op0=ALU.mult, op1=ALU.add),
                    reads=[cst_t, ps_s] if local else [cst_t, ps_s, gt], writes=[sb_])
                P.emit("act", lambda E: E.activation(out=pT_.a[:], in_=sb_.a[:], func=AF.Exp, bias=cb),
                       reads=[sb_, lt] if local else [sb_, gt], writes=[pT_])

        def fin_part1(j):
            F = Ft[j % 2]
            Fa = F.a
            P.emit("dve", lambda E: E.reciprocal(out=Fa[:, 0:4], in_=o1.a[:, :, 256]), reads=[o1], writes=[F])
            P.emit("dve", lambda E: E.reciprocal(out=Fa[:, 4:8], in_=o2.a[:, :, 256]), reads=[o2], pwrites=[F])
            P.emit("dve", lambda E: E.tensor_scalar(out=Fa[:, 4:8], in0=Fa[:, 4:8], scalar1=NEGLAM, scalar2=None, op0=ALU.mult),
                   reads=[F, lam_t], pwrites=[F])
            for qc in range(4):
                a_ = av[qc]
                P.emit("dve", lambda E, qc=qc, a_=a_: E.tensor_scalar(
                    out=a_.a[:], in0=o1.a[:, qc, 0:256], scalar1=Fa[:, qc:qc + 1], scalar2=None, op0=ALU.mult),
                    reads=[o1, F], writes=[a_])
                P.emit("dve", lambda E, qc=qc, a_=a_: E.scalar_tensor_tensor(
                    out=a_.a[:], in0=o2.a[:, qc, 0:256], scalar=Fa[:, 4 + qc:5 + qc], in1=a_.a[:], op0=ALU.mult, op1=ALU.add),
                    reads=[o2, F, a_], writes=[a_])
                P.emit("act", lambda E, qc=qc, a_=a_: E.activation(out=sqj.a[:], in_=a_.a[:], func=AF.Square,
                                                                  accum_out=Fa[:, 8 + qc:9 + qc]), reads=[a_], writes=[sqj], pwrites=[F])
            P.emit("act", lambda E: E.activation(out=Fa[:, 8:12], in_=Fa[:, 8:12], func=AF.Ln, scale=1.0 / 256, bias=EPS_COL),
                   reads=[F, cst_t], pwrites=[F])
            P.emit("act", lambda E: E.activation(out=Fa[:, 8:12], in_=Fa[:, 8:12], func=AF.Exp, scale=-0.5, bias=LN_SUBSCALE),
                   reads=[F, cst_t], pwrites=[F])
            for qc in range(4):
                a_, n_ = av[qc], an[qc]
                P.emit("dve", lambda E, qc=qc, a_=a_, n_=n_: E.scalar_tensor_tensor(
                    out=n_.a[:], in0=a_.a[:], scalar=Fa[:, 8 + qc:9 + qc], in1=gsub_t.a[:], op0=ALU.mult, op1=ALU.mult),
                    reads=[a_, F, gsub_t], writes=[n_])

        def fin_part2(j):
            as_ = ast[j % 2]
            pbt = PS[4 + gctr[1] % 4]
            gctr[1] += 1
            pbv = pbt.a.bitcast(BF16)
            for qc in range(4):
                for e in range(2):
                    c0 = (qc * 2 + e) * 128
                    P.emit("pe", lambda E, qc=qc, e=e, c0=c0: E.transpose(
                        out=pbv[:, c0:c0 + 128], in_=an[qc].a[:, e * 128:(e + 1) * 128], identity=ident.a[:]),
                        reads=[an[qc], ident], writes=[pbt] if (qc == 0 and e == 0) else [],
                        pwrites=[] if (qc == 0 and e == 0) else [pbt])
            pv4 = pbv.rearrange("p (q e t) -> p q e t", q=4, e=2)
            for e in range(2):
                P.emit("act", lambda E, e=e: E.activation(
                    out=as_.a[:, e, :].rearrange("p (q t) -> p q t", q=4), in_=pv4[:, :, e, :], func=AF.Copy),
                    reads=[pbt], pwrites=[as_])
            P.emit("sync", lambda E, h=h, j=j: E.dma_start(
                out=mixT_d[2 * h:2 * h + 2, :, 512 * j:512 * (j + 1)].rearrange("c p t -> p c t"), in_=as_.a[:]),
                reads=[as_], pwrites=[B_mixT], dma=as_)

        def stage2(it_):
            pT_, t, m, j, ti, n = it_["pT"], it_["t"], it_["m"], it_["j"], it_["ti"], it_["n"]
            first, last = ti == 0, ti == n - 1
            for qc in range(4):
                P.emit("pe", lambda E, qc=qc: E.matmul(
                    PS[qc].a[:, 0:257], lhsT=pT_.a[:, qc * 128:(qc + 1) * 128], rhs=vs.a[:, t, :], start=first, stop=last),
                    reads=[pT_, Bv[chunk_of(t)]], writes=[PS[qc]] if first else [], pwrites=[] if first else [PS[qc]])
            if last:
                od = o1 if m == 0 else o2
                for qc in range(4):
                    if qc % 2 == 0:
                        P.emit("act", lambda E, qc=qc, od=od: E.activation(out=od.a[:, qc, :], in_=PS[qc].a[:, 0:257], func=AF.Copy),
                               reads=[PS[qc]], pwrites=[od])
                    else:
                        P.emit("dve", lambda E, qc=qc, od=od: E.tensor_copy(out=od.a[:, qc, :], in_=PS[qc].a[:, 0:257]),
                               reads=[PS[qc]], pwrites=[od])
                if m == 1:
                    fin_part1(j)
                    return j
            return None

        pending = []
        nit = len(items)
        sufmin = [10 ** 9] * (nit + 1)
        for q_ in range(nit - 1, -1, -1):
            sufmin[q_] = min(sufmin[q_ + 1], items[q_]["t"])
        nxt_done = 0
        for idx in range(nit + LA):
            if idx < nit:
                stage1(items[idx])
            if idx >= LA:
                r = stage2(items[idx - LA])
                if r is not None:
                    pending.append((idx + DEFER, r))
                if h + 1 < NH:
                    while nxt_done < 4 and CHK[nxt_done][1] <= sufmin[idx - LA + 1]:
                        kv_load(h + 1, nxt_done)
                        nxt_done += 1
            while pending and pending[0][0] <= idx:
                fin_part2(pending.pop(0)[1])
        if h + 1 < NH:
            while nxt_done < 4:
                kv_load(h + 1, nxt_done)
                nxt_done += 1
        while pending:
            fin_part2(pending.pop(0)[1])
    P.barrier()
    if stop_after == "E":
        P.finish()
        return nc

    mx = P.tile("mx", [128, KC, 1024], BF16)
    wo = [P.tile("wo", [128, KC, 512], BF16) for _ in range(2)]
    xr = [P.tile("xr", [128, 512], F32) for _ in range(2)]
    ho = [P.tile("ho", [128, 512], F32) for _ in range(2)]
    it = 0
    for half in range(2):
        P.emit("sync", lambda E, half=half: E.dma_start(
            out=mx.a[:], in_=mixT_d[:, :, 1024 * half:1024 * (half + 1)].rearrange("k p t -> p k t")),
            reads=[B_mixT], writes=[mx], dma=mx)
        for g in range(8):
            w_ = wo[(half * 8 + g) % 2]
            wload(w_, w_out, 512 * g, 512)
            for s in range(8):
                tok0 = 1024 * half + 128 * s
                pb = PS[it % 4]
                x_, o_ = xr[it % 2], ho[it % 2]
                it += 1
                P.emit("sync", lambda E, x_=x_, tok0=tok0, g=g: E.dma_start(
                    out=x_.a[:], in_=xw[WMAX + tok0:WMAX + tok0 + 128, 512 * g:512 * (g + 1)]), writes=[x_], dma=x_)
                for kc in range(KC):
                    P.emit("pe", lambda E, pb=pb, w_=w_, s=s, kc=kc: E.matmul(
                        pb.a[:], lhsT=mx.a[:, kc, s * 128:(s + 1) * 128], rhs=w_.a[:, kc, :], start=(kc == 0), stop=(kc == KC - 1)),
                        reads=[mx, w_], writes=[pb] if kc == 0 else [], pwrites=[] if kc == 0 else [pb])
                P.emit("dve", lambda E, pb=pb, x_=x_, o_=o_: E.tensor_tensor(out=o_.a[:], in0=pb.a[:], in1=x_.a[:], op=ALU.add),
                       reads=[pb, x_], writes=[o_])
                P.emit("sync", lambda E, o_=o_, tok0=tok0, g=g: E.dma_start(
                    out=h1_d[tok0:tok0 + 128, 512 * g:512 * (g + 1)], in_=o_.a[:]), reads=[o_], pwrites=[B_h1], dma=o_)
    P.barrier()
    norm_T([(h1_d, TQ // 128, hn2T_d, B_hn2T)], g_ffn, src_buf=B_h1)
    if stop_after == "F":
        P.finish()
        return nc

    for blk in range(4):
        hb2 = P.tile("hb2", [128, KC, 512], BF16)
        actT = P.tile("actT", [128, FC, 512], BF16)
        mark = P.off
        wg = [P.tile("wg", [128, KC, 256], BF16) for _ in range(2)]
        wu = [P.tile("wu", [128, KC, 256], BF16) for _ in range(2)]
        sgl = [P.tile("sgl", [128, 512], F32) for _ in range(2)]
        P.emit("sync", lambda E, blk=blk, hb2=hb2: E.dma_start(
            out=hb2.a[:], in_=hn2T_d[blk]),
            reads=[B_hn2T[blk]], writes=[hb2], dma=hb2)
        for g in range(DFF // 256):
            wg_, wu_ = wg[g % 2], wu[g % 2]
            wload(wg_, w_gate, 256 * g, 256)
            wload(wu_, w_up, 256 * g, 256)
            for c in range(2):
                fc = 2 * g + c
                pg, pu = PS[(fc % 2) * 2], PS[(fc % 2) * 2 + 1]
                for (pb, w_) in ((pg, wg_), (pu, wu_)):
                    for kc in range(KC):
                        P.emit("pe", lambda E, pb=pb, w_=w_, c=c, kc=kc, hb2=hb2: E.matmul(
                            pb.a[:], lhsT=w_.a[:, kc, c * 128:(c + 1) * 128], rhs=hb2.a[:, kc, :], start=(kc == 0), stop=(kc == KC - 1)),
                            reads=[w_, hb2], writes=[pb] if kc == 0 else [], pwrites=[] if kc == 0 else [pb])
                s_ = sgl[fc % 2]
                P.emit("act", lambda E, pg=pg, s_=s_: E.activation(out=s_.a[:], in_=pg.a[:], func=AF.Silu), reads=[pg], writes=[s_])
                P.emit("dve", lambda E, pu=pu, s_=s_, fc=fc, actT=actT: E.tensor_tensor(
                    out=actT.a[:, fc, :], in0=pu.a[:], in1=s_.a[:], op=ALU.mult), reads=[pu, s_], pwrites=[actT])
        P.barrier(keep=mark)
        wd = [P.tile("wd", [128, 8, 512], BF16) for _ in range(3)]
        hr = [P.tile("hr", [128, 512], F32) for _ in range(2)]
        ho2 = [P.tile("ho2", [128, 512], F32) for _ in range(2)]
        wdi = 0
        it = 0
        for g in range(8):
            for fb in range(0, FC, 8):
                nf = min(8, FC - fb)
                w_ = wd[wdi % 3]
                wdi += 1
                wload(w_, w_down, 512 * g, 512, r0=fb * 128, nk=nf)
                for k in range(nf):
                    fc = fb + k
                    for s in range(4):
                        pb = PS[(g % 2) * 4 + s]
                        P.emit("pe", lambda E, pb=pb, w_=w_, k=k, fc=fc, s=s, actT=actT: E.matmul(
                            pb.a[:], lhsT=actT.a[:, fc, s * 128:(s + 1) * 128], rhs=w_.a[:, k, :], start=(fc == 0), stop=(fc == FC - 1)),
                            reads=[actT, w_], writes=[pb] if fc == 0 else [], pwrites=[] if fc == 0 else [pb])
            for s in range(4):
                pb = PS[(g % 2) * 4 + s]
                tok0 = 512 * blk + 128 * s
                r_, o_ = hr[it % 2], ho2[it % 2]
                it += 1
                P.emit("sync", lambda E, r_=r_, tok0=tok0, g=g: E.dma_start(
                    out=r_.a[:], in_=h1_d[tok0:tok0 + 128, 512 * g:512 * (g + 1)]), reads=[B_h1], writes=[r_], dma=r_)
                P.emit("dve", lambda E, pb=pb, r_=r_, o_=o_: E.tensor_tensor(out=o_.a[:], in0=pb.a[:], in1=r_.a[:], op=ALU.add),
                       reads=[pb, r_], writes=[o_])
                P.emit("sync", lambda E, o_=o_, tok0=tok0, g=g: E.dma_start(
                    out=h2_d[tok0:tok0 + 128, 512 * g:512 * (g + 1)], in_=o_.a[:]), reads=[o_], pwrites=[B_h2], dma=o_)
        P.barrier()

    gb = P.tile("gbf", [128, D], F32)
    P.emit("sync", lambda E: E.dma_start(out=gb.a[:], in_=g_fin.partition_broadcast(128)), writes=[gb], dma=gb)
    xt = [P.tile("xtf", [128, D], F32) for _ in range(2)]
    sq = P.tile("sqf", [128, D], BF16)
    st = [P.tile("stf", [128, 2], F32) for _ in range(2)]
    ot = [P.tile("otf", [128, D], F32) for _ in range(2)]
    for i in range(TQ // 128):
        x_, s_, o_ = xt[i % 2], st[i % 2], ot[i % 2]
        P.emit("sync", lambda E, x_=x_, i=i: E.dma_start(out=x_.a[:], in_=h2_d[i * 128:(i + 1) * 128, :]),
               reads=[B_h2], writes=[x_], dma=x_)
        P.emit("act", lambda E, x_=x_, s_=s_: E.activation(out=sq.a[:], in_=x_.a[:], func=AF.Square, accum_out=s_.a[:, 0:1]),
               reads=[x_], writes=[sq], pwrites=[s_])
        P.emit("act", lambda E, s_=s_: E.activation(out=s_.a[:, 1:2], in_=s_.a[:, 0:1], func=AF.Ln, scale=1.0 / D, bias=EPS_COL),
               reads=[s_, cst_t], pwrites=[s_])
        P.emit("act", lambda E, s_=s_: E.activation(out=s_.a[:, 1:2], in_=s_.a[:, 1:2], func=AF.Exp, scale=-0.5),
               reads=[s_], pwrites=[s_])
        P.emit("dve", lambda E, x_=x_, s_=s_, o_=o_: E.scalar_tensor_tensor(
            out=o_.a[:], in0=x_.a[:], scalar=s_.a[:, 1:2], in1=gb.a[:], op0=ALU.mult, op1=ALU.mult),
            reads=[x_, s_, gb], writes=[o_])
        P.emit("sync", lambda E, o_=o_, i=i: E.dma_start(out=out_d[i * 128:(i + 1) * 128, :], in_=o_.a[:]),
               reads=[o_], pwrites=[B_out], dma=o_)
    P.barrier()
    P.finish()
    return nc


def _col(v, n):
    return np.ascontiguousarray(np.asarray(v, np.float32).reshape(n, 128).T)


def prepare_inputs(x, meta_tokens, norm_mix_g, w_in, lambda_q1, lambda_k1, lambda_q2, lambda_k2,
                   subln_g, b_glu, dw_kernel, dw_bias, conv_ln_g, conv_ln_b, w_out,
                   norm_ffn_g, w_gate, w_up, w_down, final_norm_g):
    f = lambda a: np.asarray(a, np.float32)
    xall = np.concatenate([f(meta_tokens), f(x)[0]], axis=0)
    x0 = f(x)[0]
    cst = np.zeros((128, 1024), np.float32)
    cst[:, 0:128] = np.eye(128, dtype=np.float32)
    cst[:, 128:256] = 1.0
    cst[:, 256:768] = np.arange(512, dtype=np.float32)[None, :] - np.arange(128, dtype=np.float32)[:, None]
    cst[:, 768] = f(lambda_q1)[0]
    cst[:, 769] = f(lambda_k1)[0]
    cst[:, 770] = f(lambda_q2)[0]
    cst[:, 771] = f(lambda_k2)[0]
    cst[:, 772] = math.log(1.0 - LAM_INIT)
    cst[:, 773] = EPS
    cst[:, 774] = LN_EPS
    qk = np.arange(512, dtype=np.float32)[None, :] - np.arange(128, dtype=np.float32)[:, None]
    adtab = np.stack([np.abs(qk - 128.0 * i) for i in range(4)], axis=1).astype(np.float32)
    lent, nl = local_entries()
    ltab = np.zeros((128, nl), np.float32)
    for lst in lent.values():
        for (t, mode, ca, cb, col) in lst:
            ltab[:, col] = cb
    dwk = np.ascontiguousarray(f(dw_kernel)[0].T.reshape(16, 128, 31).transpose(1, 0, 2).reshape(128, 16 * 31))
    shared = {
        "w_in": np.ascontiguousarray(f(w_in)[0]), "w_out": np.ascontiguousarray(f(w_out)[0]),
        "w_gate": np.ascontiguousarray(f(w_gate)[0]), "w_up": np.ascontiguousarray(f(w_up)[0]),
        "w_down": np.ascontiguousarray(f(w_down)[0]),
        "g_mix": np.ascontiguousarray(f(norm_mix_g)[0]), "g_ffn": np.ascontiguousarray(f(norm_ffn_g)[0]),
        "g_fin": np.ascontiguousarray(f(final_norm_g)), "g_sub": np.ascontiguousarray(f(subln_g)[0]),
        "cst": cst, "dwk": dwk, "adtab": np.ascontiguousarray(adtab), "ltab": ltab,
    }
    maps = []
    for c in range(NCORES):
        qlo = NMETA + TQ * c
        pos = qlo - WMAX + np.arange(NW)
        valid = (pos >= 0) & (pos < L)
        xw = np.zeros((NW, D), np.float32)
        xw[valid] = xall[pos[valid]]
        ptab = np.zeros((128, 384), np.float32)
        ptab[:, 0:32] = _col(f(b_glu)[0], 32)
        ptab[:, 32:48] = _col(f(dw_bias)[0], 16)
        ptab[:, 48:64] = _col(f(conv_ln_g)[0], 16)
        ptab[:, 64:80] = _col(f(conv_ln_b)[0], 16)
        o_mg = 80 + NW // 128
        o_em = o_mg + NG // 128
        ptab[:, 80:o_mg] = _col(valid.astype(np.float32), NW // 128)
        ptab[:, o_mg:o_em] = _col((np.arange(NG) < L).astype(np.float32), NG // 128)
        epos = qlo - 16 + np.concatenate([np.arange(16), NE - 16 + np.arange(16)])
        ptab[:, o_em:o_em + 32] = ((epos >= 0) & (epos < L)).astype(np.float32)[None, :]
        t = np.arange(NGT)[:, None]
        j = np.arange(4)[None, :]
        k0 = np.where(t < 128, NMETA + (128 * t + TQ * c) % SEQ, 0)
        dl = (qlo + 512 * j - k0).astype(np.float64)
        own = np.zeros_like(dl, dtype=bool)
        gtab = np.zeros((4, 128, 1032), np.float32)
        for hh in range(4):
            sl = SLOPES[4 + hh]
            a = np.where(dl > 0, -sl, sl)
            b = -sl * np.abs(dl)
            a = np.where(own, 0.0, a)
            b = np.where(own, -30000.0, b)
            gtab[hh, :, 0:516] = a.reshape(-1)[None, :]
            gtab[hh, :, 516:1032] = b.reshape(-1)[None, :]
        m = dict(shared)
        m["xw"] = xw
        xg = np.zeros((NG, D), np.float32)
        xg[:SEQ] = np.roll(x0, -TQ * c, axis=0)
        xg[SEQ:L] = f(meta_tokens)
        m["xg"] = xg
        m["ptab"] = ptab
        m["gtab"] = gtab
        maps.append(m)
    return maps


def kernel(**inputs):
    maps = prepare_inputs(**inputs)
    nc = build_program()
    res = run_bass_kernel_spmd(nc, maps, core_ids=list(range(NCORES)))
    outs = [np.asarray(res.results[c]["out"], dtype=np.float32) for c in range(NCORES)]
    return np.concatenate(outs, axis=0)[None]
```

```python
import math
import numpy as np
import concourse.bass as bass
import concourse.mybir as mybir
from concourse.bass_utils import run_bass_kernel_spmd

F32, BF16 = mybir.dt.float32, mybir.dt.bfloat16
AF = mybir.ActivationFunctionType
ALU = mybir.AluOpType

NCORES = 8
D = 4096
KC = 32
SEQ = 16384
NMETA = 16
L = SEQ + NMETA
TQ = 2048
NH = 8
DFF = 11008
FC = DFF // 128
EPS = 1e-6
LN_EPS = 1e-5
SCALE = 128 ** -0.5
LAM_INIT = 0.2
SLOPES = [2.0 ** -(i + 1) for i in range(NH)]
WLOC = [512, 1024, 1536, 3072]
WMAX = 3072
NW = TQ + 2 * WMAX
NG = 16896
NGT = 129
NE = TQ + 32
E0 = WMAX - 16
ZTHR = 176.0
SBUF_BASE = 16576
SBUF_LIM = 229344


class Rec:
    __slots__ = ("eng", "fn", "deps", "needed", "signo", "dsem", "dcount", "idx")


class Buf:
    def __init__(self, name):
        self.name = name
        self.w = {}
        self.r = {}
        self.dsem = None


class TL:
    def __init__(self, a, b):
        self.a = a
        self.b = b


class Prog:
    CE = ("act", "dve", "pool", "pe")
    ALLE = ("sync", "act", "dve", "pool", "pe")

    def __init__(self, nc):
        self.nc = nc
        self.recs = {e: [] for e in self.ALLE}
        self.dsem_counts = []
        self.dsem_free = []
        self.phase_bufs = []
        self.off = SBUF_BASE
        self.pers_off = SBUF_BASE
        self.nt = 0

    def tile(self, name, shape, dtype, pers=False):
        nb = int(np.prod(shape[1:])) * (2 if dtype == BF16 else 4)
        nb = (nb + 63) // 64 * 64
        self.nt += 1
        t = self.nc.alloc_sbuf_tensor_at("%s_%d" % (name, self.nt), list(shape), dtype, offset=self.off)
        self.off += nb
        assert self.off <= SBUF_LIM, ("sbuf overflow", name, self.off)
        b = Buf(name)
        if pers:
            self.pers_off = self.off
        else:
            self.phase_bufs.append(b)
        return TL(t.ap(), b)

    def _ev(self, rec):
        if rec.dsem is not None:
            return ("d", rec.dsem), rec.dcount
        return rec.eng, rec

    def _add(self, rec, evmap):
        for k, v in evmap.items():
            if k == "pe" and rec.eng == "pe":
                continue
            if isinstance(k, tuple):
                if rec.deps.get(k, -1) < v:
                    rec.deps[k] = v
            else:
                o = rec.deps.get(k)
                if o is None or o.idx < v.idx:
                    rec.deps[k] = v
                v.needed = True

    def emit(self, eng, fn, reads=(), writes=(), pwrites=(), dma=None):
        rec = Rec()
        rec.eng, rec.fn, rec.deps, rec.needed, rec.signo = eng, fn, {}, False, 0
        rec.dsem, rec.dcount = None, 0
        rec.idx = len(self.recs[eng])
        if dma is not None:
            b = dma.b if isinstance(dma, TL) else dma
            if b.dsem is None:
                if self.dsem_free:
                    b.dsem = self.dsem_free.pop()
                else:
                    b.dsem = len(self.dsem_counts)
                    self.dsem_counts.append(0)
            self.dsem_counts[b.dsem] += 16
            rec.dsem, rec.dcount = b.dsem, self.dsem_counts[b.dsem]
        for t in reads:
            b = t.b if isinstance(t, TL) else t
            self._add(rec, b.w)
        for t in list(writes) + list(pwrites):
            b = t.b if isinstance(t, TL) else t
            self._add(rec, b.r)
        for t in writes:
            b = t.b if isinstance(t, TL) else t
            self._add(rec, b.w)
        k, v = self._ev(rec)
        for t in reads:
            b = t.b if isinstance(t, TL) else t
            b.r[k] = v
        for t in writes:
            b = t.b if isinstance(t, TL) else t
            b.w = {k: v}
            b.r = {}
        for t in pwrites:
            b = t.b if isinstance(t, TL) else t
            if b.r:
                b.w = {}
                b.r = {}
            b.w[k] = v
        self.recs[eng].append(rec)
        return rec

    def barrier(self, keep=None):
        lasts = {}
        for e in self.CE:
            if self.recs[e]:
                r = None
                for x in reversed(self.recs[e]):
                    if x.fn is not None and x.dsem is None:
                        r = x
                        break
                if r is not None:
                    lasts[e] = r
        for e in self.ALLE:
            rec = Rec()
            rec.eng, rec.fn, rec.deps, rec.needed, rec.signo = e, None, {}, False, 0
            rec.dsem, rec.dcount = None, 0
            rec.idx = len(self.recs[e])
            for ce, r in lasts.items():
                if ce == e and e == "pe":
                    continue
                rec.deps[ce] = r
                r.needed = True
            for i, c in enumerate(self.dsem_counts):
                if c > 0:
                    rec.deps[("d", i)] = c
            self.recs[e].append(rec)
        for b in self.phase_bufs:
            if b.dsem is not None:
                self.dsem_free.append(b.dsem)
                b.dsem = None
        self.phase_bufs = []
        self.off = self.pers_off if keep is None else keep

    def finish(self):
        nc = self.nc
        for e in self.CE:
            n = 0
            for r in self.recs[e]:
                if r.needed:
                    n += 1
                    r.signo = n
        nd = len(self.dsem_counts)
        csem = {e: nc.alloc_semaphore("cs_" + e) for e in self.CE}
        dsem = [nc.alloc_semaphore("ds_%d" % i) for i in range(nd)]
        recs = self.recs

        def run(E, ename):
            waited = {}
            for r in recs[ename]:
                for k, v in r.deps.items():
                    if isinstance(k, tuple):
                        sem, val = dsem[k[1]], v
                    else:
                        sem, val = csem[k], v.signo
                    if waited.get(k, -1) >= val:
                        continue
                    waited[k] = val
                    E.wait_ge(sem, val)
                if r.fn is None:
                    continue
                ins = r.fn(E)
                if r.needed:
                    ins.then_inc(csem[ename], 1)
                if r.dsem is not None:
                    ins.then_inc(dsem[r.dsem], 16)

        with nc.Block() as block:
            @block.sync
            def _(E):
                run(E, "sync")

            @block.scalar
            def _(E):
                run(E, "act")

            @block.vector
            def _(E):
                run(E, "dve")

            @block.gpsimd
            def _(E):
                run(E, "pool")

            @block.tensor
            def _(E):
                run(E, "pe")


def local_entries():
    ents, col = {}, 0
    for h in range(4):
        sl = SLOPES[h]
        for j in range(4):
            lst = []
            for t in range(4 * j, 4 * j + (512 + 2 * WLOC[h]) // 128):
                dl = 512 * j + WLOC[h] - 128 * t
                if (dl >= 128 and sl * (dl - 127) > ZTHR) or (dl <= -512 and sl * (-dl - 511) > ZTHR):
                    col += 1
                    continue
                if dl >= 128:
                    lst.append((t, "f", -sl, -sl * dl, col))
                elif dl <= -512:
                    lst.append((t, "f", sl, sl * dl, col))
                else:
                    lst.append((t, "d", (-dl) // 128, 0.0, col))
                col += 1
            ents[(h, j)] = lst
    return ents, col


def build_program(stop_after=None):
    nc = bass.Bass("TRN2", target_bir_lowering=False)
    P = Prog(nc)

    def din(name, shape, dt=F32):
        return nc.dram_tensor(name, list(shape), dt, kind="ExternalInput").ap()

    xw = din("xw", [NW, D])
    xg = din("xg", [NG, D])
    w_in = din("w_in", [D, 10240])
    w_out = din("w_out", [D, D])
    w_gate = din("w_gate", [D, DFF])
    w_up = din("w_up", [D, DFF])
    w_down = din("w_down", [DFF, D])
    g_mix = din("g_mix", [D])
    g_ffn = din("g_ffn", [D])
    g_fin = din("g_fin", [D])
    g_sub = din("g_sub", [256])
    cst = din("cst", [128, 1024])
    ptab = din("ptab", [128, 384])
    dwk_d = din("dwk", [128, 16 * 31])
    adtab = din("adtab", [128, 4, 512])
    gtab = din("gtab", [4, 128, 1032])
    ltab = din("ltab", [128, local_entries()[1]])
    out_d = nc.dram_tensor("out", [TQ, D], F32, kind="ExternalOutput").ap()

    hwT = nc.dram_tensor("hwT", [NW // 512, 128, KC, 512], BF16).ap()
    hgT = nc.dram_tensor("hgT", [NG // 512, 128, KC, 512], BF16).ap()
    spans = [TQ + 2 * w for w in WLOC] + [NG] * 4
    kT_d = [nc.dram_tensor("kT%d" % h, [2, 128, spans[h]], BF16).ap() for h in range(NH)]
    v_d = [nc.dram_tensor("v%d" % h, [128, spans[h] // 128, 257], BF16).ap() for h in range(NH)]
    qT_d = nc.dram_tensor("qT", [16, 128, TQ], BF16).ap()
    hglu_d = nc.dram_tensor("hglu", [16, 128, NE], BF16).ap()
    mixT_d = nc.dram_tensor("mixT", [KC, 128, TQ], BF16).ap()
    h1_d = nc.dram_tensor("h1", [TQ, D], F32).ap()
    h2_d = nc.dram_tensor("h2", [TQ, D], F32).ap()
    hn2T_d = nc.dram_tensor("hn2T", [TQ // 512, 128, KC, 512], BF16).ap()
    B_hwT = [Buf("hwT%d" % i) for i in range(NW // 512)]
    B_hgT = [Buf("hgT%d" % i) for i in range(NG // 512)]
    B_kT = [Buf("kT%d" % h) for h in range(NH)]
    B_v = [Buf("v%d" % h) for h in range(NH)]
    B_qT = Buf("qT")
    B_hglu = Buf("hglu")
    B_mixT = Buf("mixT")
    B_h1 = Buf("h1")
    B_h2 = Buf("h2")
    B_hn2T = [Buf("hn2T%d" % i) for i in range(4)]
    B_out = Buf("out")

    PS = []
    for i in range(8):
        t = nc.alloc_psum_tensor("psb%d" % i, [128, 512], F32)
        PS.append(TL(t.ap(), Buf("ps%d" % i)))

    cst_t = P.tile("cst", [128, 1024], F32, pers=True)
    ptab_t = P.tile("ptab", [128, 384], F32, pers=True)
    dwk_t = P.tile("dwk", [128, 16 * 31], F32, pers=True)
    ident = P.tile("ident", [128, 128], BF16, pers=True)
    gsub_t = P.tile("gsub", [128, 256], F32, pers=True)
    lam_t = P.tile("lamt", [128, 8], F32, pers=True)
    P.emit("sync", lambda E: E.dma_start(out=cst_t.a[:], in_=cst[:, :]), writes=[cst_t], dma=cst_t)
    P.emit("sync", lambda E: E.dma_start(out=ptab_t.a[:], in_=ptab[:, :]), writes=[ptab_t], dma=ptab_t)
    P.emit("sync", lambda E: E.dma_start(out=dwk_t.a[:], in_=dwk_d[:, :]), writes=[dwk_t], dma=dwk_t)
    P.emit("sync", lambda E: E.dma_start(out=gsub_t.a[:], in_=g_sub.partition_broadcast(128)), writes=[gsub_t], dma=gsub_t)
    P.emit("dve", lambda E: E.tensor_copy(out=ident.a[:], in_=cst_t.a[:, 0:128]), reads=[cst_t], writes=[ident])
    ones_f = cst_t.a[:, 128:256]
    D0 = cst_t.a[:, 256:768]
    PT_BGLU, PT_DWB, PT_LNG, PT_LNB, PT_MW, PT_MG, PT_EM = 0, 32, 48, 64, 80, 80 + NW // 128, 80 + NW // 128 + NG // 128
    pt = ptab_t.a
    P.emit("dve", lambda E: E.tensor_tensor(out=lam_t.a[:, 0:1], in0=cst_t.a[:, 768:769], in1=cst_t.a[:, 769:770], op=ALU.mult),
           reads=[cst_t], pwrites=[lam_t])
    P.emit("dve", lambda E: E.tensor_tensor(out=lam_t.a[:, 1:2], in0=cst_t.a[:, 770:771], in1=cst_t.a[:, 771:772], op=ALU.mult),
           reads=[cst_t], pwrites=[lam_t])
    P.emit("pe", lambda E: E.matmul(PS[0].a[:, 0:2], lhsT=ones_f, rhs=lam_t.a[:, 0:2], start=True, stop=True),
           reads=[cst_t, lam_t], writes=[PS[0]])
    P.emit("act", lambda E: E.activation(out=lam_t.a[:, 2:4], in_=PS[0].a[:, 0:2], func=AF.Exp), reads=[PS[0]], pwrites=[lam_t])
    P.emit("dve", lambda E: E.tensor_tensor(out=lam_t.a[:, 4:5], in0=lam_t.a[:, 2:3], in1=lam_t.a[:, 3:4], op=ALU.subtract),
           reads=[lam_t], pwrites=[lam_t])
    P.emit("dve", lambda E: E.tensor_scalar(out=lam_t.a[:, 5:6], in0=lam_t.a[:, 4:5], scalar1=LAM_INIT, scalar2=-1.0,
                                            op0=ALU.add, op1=ALU.mult), reads=[lam_t], pwrites=[lam_t])
    NEGLAM = lam_t.a[:, 5:6]
    LN_SUBSCALE = cst_t.a[:, 772:773]
    EPS_COL = cst_t.a[:, 773:774]
    LNEPS_COL = cst_t.a[:, 774:775]

    def norm_T(srcs, g_dram, src_buf=None):
        tl = [(a, i, d, b) for (a, n, d, b) in srcs for i in range(n)]
        ntiles = len(tl)
        gb = P.tile("gb", [128, D], F32)
        P.emit("sync", lambda E: E.dma_start(out=gb.a[:], in_=g_dram.partition_broadcast(128)), writes=[gb], dma=gb)
        xt = [P.tile("xt", [128, D], F32) for _ in range(3)]
        sq = P.tile("sq", [128, D], BF16)
        xs = [P.tile("xs", [128, D], BF16) for _ in range(2)]
        st = [P.tile("st", [128, 2], F32) for _ in range(2)]
        hst = [P.tile("hst", [128, KC, 512], BF16) for _ in range(2)]
        psb = PS[0:4]
        def stage0(i):
            x_ = xt[i % 3]
            rd = [src_buf] if src_buf is not None else []
            src, li = tl[i][0], tl[i][1]
            P.emit("sync", lambda E, x_=x_, li=li, src=src: E.dma_start(out=x_.a[:], in_=src[li * 128:(li + 1) * 128, :]),
                   reads=rd, writes=[x_], dma=x_)

        def stage1(i):
            x_, s_, xs_ = xt[i % 3], st[i % 2], xs[i % 2]
            P.emit("act", lambda E, x_=x_, s_=s_: E.activation(out=sq.a[:], in_=x_.a[:], func=AF.Square, accum_out=s_.a[:, 0:1]),
                   reads=[x_], writes=[sq], pwrites=[s_])
            P.emit("act", lambda E, s_=s_: E.activation(out=s_.a[:, 1:2], in_=s_.a[:, 0:1], func=AF.Ln, scale=1.0 / D, bias=EPS_COL),
                   reads=[s_, cst_t], pwrites=[s_])
            P.emit("act", lambda E, s_=s_: E.activation(out=s_.a[:, 1:2], in_=s_.a[:, 1:2], func=AF.Exp, scale=-0.5),
                   reads=[s_], pwrites=[s_])
            P.emit("dve", lambda E, x_=x_, s_=s_, xs_=xs_: E.scalar_tensor_tensor(
                out=xs_.a[:], in0=x_.a[:], scalar=s_.a[:, 1:2], in1=gb.a[:], op0=ALU.mult, op1=ALU.mult),
                reads=[x_, s_, gb], writes=[xs_])

        def stage2(i):
            xs_ = xs[i % 2]
            hs = hst[(i // 4) % 2]
            li, dstT, dst_bufs = tl[i][1], tl[i][2], tl[i][3]
            for g in range(4):
                pb = psb[g]
                pbv = pb.a.bitcast(BF16)
                for k in range(8):
                    kc = g * 8 + k
                    P.emit("pe", lambda E, pbv=pbv, k=k, kc=kc, xs_=xs_: E.transpose(
                        out=pbv[:, k * 128:(k + 1) * 128], in_=xs_.a[:, kc * 128:(kc + 1) * 128], identity=ident.a[:]),
                        reads=[xs_, ident], writes=[pb] if k == 0 else [], pwrites=[] if k == 0 else [pb])
                c0 = (li % 4) * 128
                if g < 2:
                    P.emit("act", lambda E, pbv=pbv, hs=hs, g=g, c0=c0: E.activation(
                        out=hs.a[:, g * 8:(g + 1) * 8, c0:c0 + 128], in_=pbv.rearrange("p (k t) -> p k t", k=8), func=AF.Copy),
                        reads=[pb], pwrites=[hs])
                else:
                    P.emit("dve", lambda E, pbv=pbv, hs=hs, g=g, c0=c0: E.tensor_copy(
                        out=hs.a[:, g * 8:(g + 1) * 8, c0:c0 + 128], in_=pbv.rearrange("p (k t) -> p k t", k=8)),
                        reads=[pb], pwrites=[hs])
            if li % 4 == 3:
                blk = li // 4
                P.emit("sync", lambda E, hs=hs, blk=blk, dstT=dstT: E.dma_start(out=dstT[blk], in_=hs.a[:]),
                       reads=[hs], pwrites=[dst_bufs[blk]], dma=hs)

        assert all(n % 4 == 0 for (_, n, _, _) in srcs)
        stage0(0)
        stage0(1)
        for i in range(ntiles + 1):
            if i + 2 < ntiles:
                stage0(i + 2)
            if i < ntiles:
                stage1(i)
            if i >= 1:
                stage2(i - 1)
        P.barrier()

    norm_T([(xw, NW // 128, hwT, B_hwT), (xg, NG // 128, hgT, B_hgT)], g_mix)
    if stop_after == "A":
        P.finish()
        return nc

    def wload(dst, wsrc, c0, ncols, r0=0, nk=KC):
        P.emit("pool", lambda E: E.dma_start(
            out=dst.a[:, 0:nk, 0:ncols],
            in_=wsrc[r0:r0 + nk * 128, c0:c0 + ncols].rearrange("(k p) c -> p k c", p=128)),
            writes=[dst], dma=dst)

    wk = [P.tile("wk", [128, KC, 256], BF16) for _ in range(2)]
    wv = [P.tile("wv", [128, KC, 256], BF16) for _ in range(2)]
    hb = [P.tile("hb", [128, KC, 512], BF16) for _ in range(3)]
    kst = [P.tile("kst", [128, 2, 512], BF16) for _ in range(2)]
    vst = [P.tile("vst", [128, 4, 257], BF16) for _ in range(2)]
    blocks = []
    sl4 = SLOPES[4]
    for h in range(NH):
        local = h < 4
        for b in range(spans[h] // 512):
            if local or b >= NG // 512:
                col0 = (WMAX - WLOC[h] + 512 * b) if local else (WMAX + 512 * (b - NG // 512))
                blocks.append((h, b, hwT, [B_hwT[col0 // 512]], col0, PT_MW + col0 // 128))
            else:
                if h == 4 and all(sl4 * ((t_ - 4 * j_ - 4) * 128 + 1) > ZTHR and sl4 * ((128 - t_ + 4 * j_ - 1) * 128 + 1) > ZTHR
                                  for t_ in range(4 * b, 4 * b + 4) for j_ in range(4)):
                    continue
                col0 = 512 * b
                blocks.append((h, b, hgT, [B_hgT[b]], col0, PT_MG + col0 // 128))

    def bload(i):
        h, b, srcT, sb_, col0, mcol = blocks[i]
        hb_ = hb[i % 3]
        assert col0 % 512 == 0
        P.emit("sync", lambda E: E.dma_start(out=hb_.a[:], in_=srcT[col0 // 512]), reads=sb_, writes=[hb_], dma=hb_)

    bload(0)
    bload(1)
    for it in range(len(blocks)):
        h, b, srcT, sb_, col0, mcol = blocks[it]
        wk_, wv_ = wk[h % 2], wv[h % 2]
        if b == 0:
            wload(wk_, w_in, 2048 + 256 * h, 256)
            wload(wv_, w_in, 4096 + 256 * h, 256)
        if it + 2 < len(blocks):
            bload(it + 2)
        if True:
            hb_, ks_, vs_ = hb[it % 3], kst[it % 2], vst[it % 2]
            for m in range(2):
                pb = PS[(it * 2 + m) % 4]
                for kc in range(KC):
                    P.emit("pe", lambda E, pb=pb, wk_=wk_, hb_=hb_, m=m, kc=kc: E.matmul(
                        pb.a[:], lhsT=wk_.a[:, kc, m * 128:(m + 1) * 128], rhs=hb_.a[:, kc, :],
                        start=(kc == 0), stop=(kc == KC - 1)),
                        reads=[wk_, hb_], writes=[pb] if kc == 0 else [], pwrites=[] if kc == 0 else [pb])
                P.emit("act", lambda E, pb=pb, ks_=ks_, m=m: E.activation(out=ks_.a[:, m, :], in_=pb.a[:], func=AF.Copy),
                       reads=[pb], pwrites=[ks_])
            P.emit("sync", lambda E, ks_=ks_, h=h, b=b: E.dma_start(
                out=kT_d[h][:, :, b * 512:(b + 1) * 512].rearrange("m p t -> p m t"), in_=ks_.a[:]),
                reads=[ks_], pwrites=[B_kT[h]], dma=ks_)
            for s in range(4):
                pb = PS[4 + (it * 4 + s) % 4]
                for kc in range(KC):
                    P.emit("pe", lambda E, pb=pb, wv_=wv_, hb_=hb_, s=s, kc=kc: E.matmul(
                        pb.a[:, 0:256], lhsT=hb_.a[:, kc, s * 128:(s + 1) * 128], rhs=wv_.a[:, kc, :],
                        start=(kc == 0), stop=(kc == KC - 1)),
                        reads=[wv_, hb_], writes=[pb] if kc == 0 else [], pwrites=[] if kc == 0 else [pb])
                P.emit("dve", lambda E, pb=pb, vs_=vs_, s=s: E.tensor_copy(out=vs_.a[:, s, 0:256], in_=pb.a[:, 0:256]),
                       reads=[pb], pwrites=[vs_])
                P.emit("dve", lambda E, vs_=vs_, s=s, mc=mcol + s: E.tensor_copy(out=vs_.a[:, s, 256:257], in_=pt[:, mc:mc + 1]),
                       reads=[ptab_t], pwrites=[vs_])
            P.emit("sync", lambda E, vs_=vs_, h=h, b=b: E.dma_start(
                out=v_d[h][:, 4 * b:4 * b + 4, :], in_=vs_.a[:]),
                reads=[vs_], pwrites=[B_v[h]], dma=vs_)
    P.barrier()

    HE = NE // 2
    hq = P.tile("hq", [128, KC, HE], BF16)
    wq = [P.tile("wq", [128, KC, 512], BF16) for _ in range(2)]
    qst = [P.tile("qst", [128, 512], BF16) for _ in range(2)]
    ua = P.tile("ua", [128, 4, HE], F32)
    sg = [P.tile("sg", [128, 416], F32) for _ in range(2)]
    hgs = [P.tile("hgs", [128, HE], BF16) for _ in range(2)]
    UT = [(0, 416), (416, 416), (832, 208)]
    wi = 0
    for half in range(2):
        e0 = half * HE
        c_lo = E0 + e0
        pos = 0
        first = True
        while pos < HE:
            bk, bo = (c_lo + pos) // 512, (c_lo + pos) % 512
            n_ = min(512 - bo, HE - pos)
            P.emit("sync", lambda E, pos=pos, bk=bk, bo=bo, n_=n_: E.dma_start(
                out=hq.a[:, :, pos:pos + n_], in_=hwT[bk][:, :, bo:bo + n_]),
                reads=[B_hwT[bk]], writes=[hq] if first else [], pwrites=[] if first else [hq], dma=hq)
            first = False
            pos += n_
        qoff = 16 if half == 0 else 0
        for g in range(4):
            w_ = wq[wi % 2]
            wi += 1
            wload(w_, w_in, 512 * g, 512)
            for c in range(4):
                for nb in range(2):
                    pb = PS[(c * 2 + nb) % 8]
                    for kc in range(KC):
                        P.emit("pe", lambda E, pb=pb, w_=w_, c=c, nb=nb, kc=kc, qoff=qoff: E.matmul(
                            pb.a[:], lhsT=w_.a[:, kc, c * 128:(c + 1) * 128],
                            rhs=hq.a[:, kc, qoff + nb * 512:qoff + (nb + 1) * 512], start=(kc == 0), stop=(kc == KC - 1)),
                            reads=[w_, hq], writes=[pb] if kc == 0 else [], pwrites=[] if kc == 0 else [pb])
                    q_ = qst[(c * 2 + nb) % 2]
                    P.emit("act", lambda E, pb=pb, q_=q_: E.activation(out=q_.a[:], in_=pb.a[:], func=AF.Copy, scale=SCALE),
                           reads=[pb], writes=[q_])
                    t0 = half * 1024 + nb * 512
                    P.emit("sync", lambda E, q_=q_, cc=g * 4 + c, t0=t0: E.dma_start(out=qT_d[cc, :, t0:t0 + 512], in_=q_.a[:]),
                           reads=[q_], pwrites=[B_qT], dma=q_)
        for j in range(4):
            for part in range(2):
                w_ = wq[wi % 2]
                wi += 1
                wload(w_, w_in, 6144 + 2048 * part + 512 * j, 512)
                for c in range(4):
                    ch = 4 * j + c
                    hg_ = hgs[ch % 2]
                    for ti, (n0, nn) in enumerate(UT):
                        pb = PS[(c * 3 + ti) % 8]
                        for kc in range(KC):
                            P.emit("pe", lambda E, pb=pb, w_=w_, c=c, kc=kc, n0=n0, nn=nn: E.matmul(
                                pb.a[:, 0:nn], lhsT=w_.a[:, kc, c * 128:(c + 1) * 128], rhs=hq.a[:, kc, n0:n0 + nn],
                                start=(kc == 0), stop=(kc == KC - 1)),
                                reads=[w_, hq], writes=[pb] if kc == 0 else [], pwrites=[] if kc == 0 else [pb])
                        if part == 0:
                            P.emit("act", lambda E, pb=pb, c=c, n0=n0, nn=nn, ch=ch: E.activation(
                                out=ua.a[:, c, n0:n0 + nn], in_=pb.a[:, 0:nn], func=AF.Identity,
                                bias=pt[:, PT_BGLU + ch:PT_BGLU + ch + 1]), reads=[pb, ptab_t], pwrites=[ua])
                        else:
                            s_ = sg[ti % 2]
                            P.emit("act", lambda E, pb=pb, s_=s_, nn=nn, ch=ch: E.activation(
                                out=s_.a[:, 0:nn], in_=pb.a[:, 0:nn], func=AF.Sigmoid,
                                bias=pt[:, PT_BGLU + 16 + ch:PT_BGLU + 16 + ch + 1]), reads=[pb, ptab_t], writes=[s_])
                            P.emit("dve", lambda E, s_=s_, c=c, n0=n0, nn=nn, hg_=hg_: E.tensor_tensor(
                                out=hg_.a[:, n0:n0 + nn], in0=ua.a[:, c, n0:n0 + nn], in1=s_.a[:, 0:nn], op=ALU.mult),
                                reads=[s_, ua], pwrites=[hg_])
                    if part == 1:
                        hc = 0 if half == 0 else HE - 16
                        mc = PT_EM + 16 * half
                        P.emit("dve", lambda E, hg_=hg_, hc=hc, mc=mc: E.tensor_tensor(
                            out=hg_.a[:, hc:hc + 16], in0=hg_.a[:, hc:hc + 16], in1=pt[:, mc:mc + 16], op=ALU.mult),
                            reads=[hg_, ptab_t], pwrites=[hg_])
                        P.emit("sync", lambda E, hg_=hg_, ch=ch, e0=e0: E.dma_start(out=hglu_d[ch, :, e0:e0 + HE], in_=hg_.a[:]),
                               reads=[hg_], pwrites=[B_hglu], dma=hg_)
    P.barrier()
    if stop_after == "C":
        P.finish()
        return nc

    hin = [P.tile("hin", [128, 16, 544], BF16) for _ in range(2)]
    dg = [P.tile("dg", [128, 31, 128], BF16) for _ in range(2)]
    yc2 = [[P.tile("y", [128, 512], F32) for _ in range(16)] for _ in range(2)]
    ysq = [P.tile("ysq", [128, 512], F32) for _ in range(2)]
    mst = P.tile("mst", [128, 4, 512], F32)
    t1 = [P.tile("t1", [128, 512], F32) for _ in range(2)]
    cst_ = [P.tile("cstg", [128, 16, 512], BF16) for _ in range(2)]
    dwk3 = dwk_t.a.rearrange("p (c j) -> p c j", j=31)

    def dload(blk):
        hi = hin[blk % 2]
        P.emit("sync", lambda E: E.dma_start(
            out=hi.a[:, :, 0:542], in_=hglu_d[:, :, 512 * blk + 1:512 * blk + 543].rearrange("c p t -> p c t")),
            reads=[B_hglu], writes=[hi], dma=hi)

    dload(0)
    for blk in range(4):
        hi = hin[blk % 2]
        yc = yc2[blk % 2]
        if blk + 1 < 4:
            dload(blk + 1)
        for ch in range(16):
            dg_ = dg[ch % 2]
            for j in range(31):
                P.emit("dve", lambda E, dg_=dg_, ch=ch, j=j: E.tensor_scalar(
                    out=dg_.a[:, j, :], in0=ident.a[:], scalar1=dwk3[:, ch, j:j + 1], scalar2=None, op0=ALU.mult),
                    reads=[ident, dwk_t], pwrites=[dg_])
            pb = PS[2 + ch % 4]
            for j in range(31):
                P.emit("pe", lambda E, pb=pb, dg_=dg_, hi=hi, ch=ch, j=j: E.matmul(
                    pb.a[:], lhsT=dg_.a[:, j, :], rhs=hi.a[:, ch, j:j + 512], start=(j == 0), stop=(j == 30)),
                    reads=[dg_, hi], writes=[pb] if j == 0 else [], pwrites=[] if j == 0 else [pb])
            P.emit("act", lambda E, pb=pb, ch=ch, yc=yc: E.activation(
                out=yc[ch].a[:], in_=pb.a[:], func=AF.Identity, bias=pt[:, PT_DWB + ch:PT_DWB + ch + 1]),
                reads=[pb, ptab_t], writes=[yc[ch]])
            P.emit("pe", lambda E, ch=ch, yc=yc: E.matmul(PS[0].a[:], lhsT=ones_f, rhs=yc[ch].a[:], start=(ch == 0), stop=(ch == 15)),
                   reads=[yc[ch], cst_t], writes=[PS[0]] if ch == 0 else [], pwrites=[] if ch == 0 else [PS[0]])
            q_ = ysq[ch % 2]
            P.emit("act", lambda E, ch=ch, q_=q_, yc=yc: E.activation(out=q_.a[:], in_=yc[ch].a[:], func=AF.Square),
                   reads=[yc[ch]], writes=[q_])
            P.emit("pe", lambda E, ch=ch, q_=q_: E.matmul(PS[1].a[:], lhsT=ones_f, rhs=q_.a[:], start=(ch == 0), stop=(ch == 15)),
                   reads=[q_, cst_t], writes=[PS[1]] if ch == 0 else [], pwrites=[] if ch == 0 else [PS[1]])
        m_ = mst.a
        P.emit("dve", lambda E: E.tensor_scalar(out=m_[:, 0, :], in0=PS[0].a[:], scalar1=1.0 / 2048, scalar2=None, op0=ALU.mult),
               reads=[PS[0]], pwrites=[mst])
        P.emit("dve", lambda E: E.tensor_tensor(out=m_[:, 2, :], in0=m_[:, 0, :], in1=m_[:, 0, :], op=ALU.mult),
               reads=[mst], pwrites=[mst])
        P.emit("dve", lambda E: E.scalar_tensor_tensor(out=m_[:, 1, :], in0=PS[1].a[:], scalar=1.0 / 2048, in1=m_[:, 2, :],
                                                       op0=ALU.mult, op1=ALU.subtract), reads=[PS[1], mst], pwrites=[mst])
        P.emit("act", lambda E: E.activation(out=m_[:, 1, :], in_=m_[:, 1, :], func=AF.Ln, bias=LNEPS_COL),
               reads=[mst, cst_t], pwrites=[mst])
        P.emit("act", lambda E: E.activation(out=m_[:, 1, :], in_=m_[:, 1, :], func=AF.Exp, scale=-0.5), reads=[mst], pwrites=[mst])
        P.emit("dve", lambda E: E.scalar_tensor_tensor(out=m_[:, 2, :], in0=m_[:, 0, :], scalar=-1.0, in1=m_[:, 1, :],
                                                       op0=ALU.mult, op1=ALU.mult), reads=[mst], pwrites=[mst])
        cs = cst_[blk % 2]
        for ch in range(16):
            tt = t1[ch % 2]
            P.emit("dve", lambda E, ch=ch, tt=tt, yc=yc: E.tensor_tensor(out=tt.a[:], in0=yc[ch].a[:], in1=m_[:, 1, :], op=ALU.mult),
                   reads=[yc[ch], mst], writes=[tt])
            P.emit("dve", lambda E, tt=tt: E.tensor_tensor(out=tt.a[:], in0=tt.a[:], in1=m_[:, 2, :], op=ALU.add),
                   reads=[tt, mst], writes=[tt])
            P.emit("act", lambda E, ch=ch, tt=tt, cs=cs: E.activation(
                out=cs.a[:, ch, :], in_=tt.a[:], func=AF.Silu, scale=pt[:, PT_LNG + ch:PT_LNG + ch + 1],
                bias=pt[:, PT_LNB + ch:PT_LNB + ch + 1]), reads=[tt, ptab_t], pwrites=[cs])
        P.emit("sync", lambda E, cs=cs, blk=blk: E.dma_start(
            out=mixT_d[16:32, :, 512 * blk:512 * (blk + 1)].rearrange("c p t -> p c t"), in_=cs.a[:]),
            reads=[cs], pwrites=[B_mixT], dma=cs)
    P.barrier()
    if stop_after == "D":
        P.finish()
        return nc

    NKT = NGT
    LA = 3
    DEFER = 8
    kTs = P.tile("kTs", [128, 2, NKT * 128], BF16)
    vs = P.tile("vs", [128, NKT, 257], BF16)
    qs = P.tile("qs", [128, 2, TQ], BF16)
    gt = P.tile("gt", [128, 1032], F32)
    adt = P.tile("adt", [128, 4, 512], F32)
    LENT, nlcol = local_entries()
    lt = P.tile("lt", [128, nlcol], F32)
    P.emit("sync", lambda E: E.dma_start(out=lt.a[:], in_=ltab[:, :]), writes=[lt], dma=lt)
    sbb = [P.tile("sbb", [128, 512], F32) for _ in range(4)]
    pTb = [P.tile("pTb", [128, 512], BF16) for _ in range(4)]
    o1 = P.tile("o1", [128, 4, 257], F32)
    o2 = P.tile("o2", [128, 4, 257], F32)
    Ft = [P.tile("Ft", [128, 16], F32) for _ in range(2)]
    av = [P.tile("av", [128, 256], F32) for _ in range(4)]
    an = [P.tile("an", [128, 256], BF16) for _ in range(4)]
    sqj = P.tile("sqj", [128, 256], BF16)
    ast = [P.tile("ast", [128, 2, 512], BF16) for _ in range(2)]
    P.emit("sync", lambda E: E.dma_start(out=adt.a[:], in_=adtab[:, :, :]), writes=[adt], dma=adt)
    gctr = [0, 0]
    CHK = [(0, 37), (37, 74), (74, 111), (111, NKT)]
    Bk = [Buf("kch%d" % c) for c in range(4)]
    Bv = [Buf("vch%d" % c) for c in range(4)]
    P.phase_bufs.extend(Bk + Bv)

    def chunk_of(t):
        return min(t // 37, 3)

    def kv_load(h, c):
        lo, hi = CHK[c]
        if h < 4:
            segs = [(lo, min(hi, spans[h] // 128), lo)]
        else:
            segs = [(lo, min(hi, NGT), lo), (max(lo, NGT), hi, NG // 128 + max(lo, NGT) - NGT)]
        first = True
        for (a, b_, s0) in segs:
            if b_ <= a:
                continue
            P.emit("sync", lambda E, h=h, a=a, b_=b_, s0=s0: E.dma_start(
                out=kTs.a[:, :, a * 128:b_ * 128], in_=kT_d[h][:, :, s0 * 128:(s0 + b_ - a) * 128].rearrange("m p t -> p m t")),
                reads=[B_kT[h]], writes=[Bk[c]] if first else [], pwrites=[] if first else [Bk[c]], dma=Bk[c])
            P.emit("sync", lambda E, h=h, a=a, b_=b_, s0=s0: E.dma_start(
                out=vs.a[:, a:b_, :], in_=v_d[h][:, s0:s0 + b_ - a, :]),
                reads=[B_v[h]], writes=[Bv[c]] if first else [], pwrites=[] if first else [Bv[c]], dma=Bv[c])
            first = False

    for c in range(4):
        kv_load(0, c)
    for h in range(NH):
        local = h < 4
        sl = SLOPES[h]
        if not local:
            P.emit("sync", lambda E, h=h: E.dma_start(out=gt.a[:], in_=gtab[h - 4, :, :]), writes=[gt], dma=gt)
        P.emit("sync", lambda E, h=h: E.dma_start(out=qs.a[:], in_=qT_d[2 * h:2 * h + 2, :, :].rearrange("m p t -> p m t")),
               reads=[B_qT], writes=[qs], dma=qs)
        items = []
        for j in range(4):
            ents = []
            if local:
                for (t, mode, ca, cb, col) in LENT[(h, j)]:
                    ents.append((t, mode, ca, lt.a[:, col:col + 1]))
            else:
                for t in range(NGT):
                    c = t * 4 + j
                    if 4 * j <= t < 4 * j + 4:
                        ents.append((t, "d", t - 4 * j, 0.0))
                    elif (16 <= t < 128 and sl * ((t - 4 * j - 4) * 128 + 1) > ZTHR
                          and sl * ((128 - t + 4 * j - 1) * 128 + 1) > ZTHR):
                        continue
                    else:
                        ents.append((t, "f", gt.a[:, c:c + 1], gt.a[:, 516 + c:516 + c + 1]))
            for m in range(2):
                for ti, (t, mode, ca, cb) in enumerate(ents):
                    items.append(dict(j=j, m=m, ti=ti, n=len(ents), t=t, mode=mode, ca=ca, cb=cb))

        def stage1(it_):
            g, g2 = gctr[0], gctr[1]
            gctr[0] += 1
            gctr[1] += 1
            ps_s, sb_, pT_ = PS[4 + g2 % 4], sbb[g % 4], pTb[g % 4]
            it_["pT"] = pT_
            t, m, j, mode, ca, cb = it_["t"], it_["m"], it_["j"], it_["mode"], it_["ca"], it_["cb"]
            nsl = -sl
            P.emit("pe", lambda E: E.matmul(
                ps_s.a[:], lhsT=kTs.a[:, m, t * 128:(t + 1) * 128], rhs=qs.a[:, m, j * 512:(j + 1) * 512],
                start=True, stop=True), reads=[Bk[chunk_of(t)], qs], writes=[ps_s])
            if mode == "d":
                P.emit("dve", lambda E: E.scalar_tensor_tensor(
                    out=sb_.a[:], in0=adt.a[:, ca, :], scalar=nsl, in1=ps_s.a[:], op0=ALU.mult, op1=ALU.add),
                    reads=[adt, ps_s], writes=[sb_])
                P.emit("act", lambda E: E.activation(out=pT_.a[:], in_=sb_.a[:], func=AF.Exp), reads=[sb_], writes=[pT_])
            else:
                P.emit("dve", lambda E: E.scalar_tensor_tensor(
                    out=sb_.a[:], in0=D0, scalar=ca, in1=ps_s.a[:], op0=ALU.mult, op1=ALU.add),
                    reads=[cst_t, ps_s] if local else [cst_t, ps_s, gt], writes=[sb_])
                P.emit("act", lambda E: E.activation(out=pT_.a[:], in_=sb_.a[:], func=AF.Exp, bias=cb),
                       reads=[sb_, lt] if local else [sb_, gt], writes=[pT_])

        def fin_part1(j):
            F = Ft[j % 2]
            Fa = F.a
            P.emit("dve", lambda E: E.reciprocal(out=Fa[:, 0:4], in_=o1.a[:, :, 256]), reads=[o1], writes=[F])
            P.emit("dve", lambda E: E.reciprocal(out=Fa[:, 4:8], in_=o2.a[:, :, 256]), reads=[o2], pwrites=[F])
            P.emit("dve", lambda E: E.tensor_scalar(out=Fa[:, 4:8], in0=Fa[:, 4:8], scalar1=NEGLAM, scalar2=None, op0=ALU.mult),
                   reads=[F, lam_t], pwrites=[F])
            for qc in range(4):
                a_ = av[qc]
                P.emit("dve", lambda E, qc=qc, a_=a_: E.tensor_scalar(
                    out=a_.a[:], in0=o1.a[:, qc, 0:256], scalar1=Fa[:, qc:qc + 1], scalar2=None, op0=ALU.mult),
                    reads=[o1, F], writes=[a_])
                P.emit("dve", lambda E, qc=qc, a_=a_: E.scalar_tensor_tensor(
                    out=a_.a[:], in0=o2.a[:, qc, 0:256], scalar=Fa[:, 4 + qc:5 + qc], in1=a_.a[:], op0=ALU.mult, op1=ALU.add),
                    reads=[o2, F, a_], writes=[a_])
                P.emit("act", lambda E, qc=qc, a_=a_: E.activation(out=sqj.a[:], in_=a_.a[:], func=AF.Square,
                                                                  accum_out=Fa[:, 8 + qc:9 + qc]), reads=[a_], writes=[sqj], pwrites=[F])
            P.emit("act", lambda E: E.activation(out=Fa[:, 8:12], in_=Fa[:, 8:12], func=AF.Ln, scale=1.0 / 256, bias=EPS_COL),
                   reads=[F, cst_t], pwrites=[F])
            P.emit("act", lambda E: E.activation(out=Fa[:, 8:12], in_=Fa[:, 8:12], func=AF.Exp, scale=-0.5, bias=LN_SUBSCALE),
                   reads=[F, cst_t], pwrites=[F])
            for qc in range(4):
                a_, n_ = av[qc], an[qc]
                P.emit("dve", lambda E, qc=qc, a_=a_, n_=n_: E.scalar_tensor_tensor(
                    out=n_.a[:], in0=a_.a[:], scalar=Fa[:, 8 + qc:9 + qc], in1=gsub_t.a[:], op0=ALU.mult, op1=ALU.mult),
                    reads=[a_, F, gsub_t], writes=[n_])

        def fin_part2(j):
            as_ = ast[j % 2]
            pbt = PS[4 + gctr[1] % 4]
            gctr[1] += 1
            pbv = pbt.a.bitcast(BF16)
            for qc in range(4):
                for e in range(2):
                    c0 = (qc * 2 + e) * 128
                    P.emit("pe", lambda E, qc=qc, e=e, c0=c0: E.transpose(
                        out=pbv[:, c0:c0 + 128], in_=an[qc].a[:, e * 128:(e + 1) * 128], identity=ident.a[:]),
                        reads=[an[qc], ident], writes=[pbt] if (qc == 0 and e == 0) else [],
                        pwrites=[] if (qc == 0 and e == 0) else [pbt])
            pv4 = pbv.rearrange("p (q e t) -> p q e t", q=4, e=2)
            for e in range(2):
                P.emit("act", lambda E, e=e: E.activation(
                    out=as_.a[:, e, :].rearrange("p (q t) -> p q t", q=4), in_=pv4[:, :, e, :], func=AF.Copy),
                    reads=[pbt], pwrites=[as_])
            P.emit("sync", lambda E, h=h, j=j: E.dma_start(
                out=mixT_d[2 * h:2 * h + 2, :, 512 * j:512 * (j + 1)].rearrange("c p t -> p c t"), in_=as_.a[:]),
                reads=[as_], pwrites=[B_mixT], dma=as_)

        def stage2(it_):
            pT_, t, m, j, ti, n = it_["pT"], it_["t"], it_["m"], it_["j"], it_["ti"], it_["n"]
            first, last = ti == 0, ti == n - 1
            for qc in range(4):
                P.emit("pe", lambda E, qc=qc: E.matmul(
                    PS[qc].a[:, 0:257], lhsT=pT_.a[:, qc * 128:(qc + 1) * 128], rhs=vs.a[:, t, :], start=first, stop=last),
                    reads=[pT_, Bv[chunk_of(t)]], writes=[PS[qc]] if first else [], pwrites=[] if first else [PS[qc]])
            if last:
                od = o1 if m == 0 else o2
                for qc in range(4):
                    if qc % 2 == 0:
                        P.emit("act", lambda E, qc=qc, od=od: E.activation(out=od.a[:, qc, :], in_=PS[qc].a[:, 0:257], func=AF.Copy),
                               reads=[PS[qc]], pwrites=[od])
                    else:
                        P.emit("dve", lambda E, qc=qc, od=od: E.tensor_copy(out=od.a[:, qc, :], in_=PS[qc].a[:, 0:257]),
                               reads=[PS[qc]], pwrites=[od])
                if m == 1:
                    fin_part1(j)
                    return j
            return None

        pending = []
        nit = len(items)
        sufmin = [10 ** 9] * (nit + 1)
        for q_ in range(nit - 1, -1, -1):
            sufmin[q_] = min(sufmin[q_ + 1], items[q_]["t"])
        nxt_done = 0
        for idx in range(nit + LA):
            if idx < nit:
                stage1(items[idx])
            if idx >= LA:
                r = stage2(items[idx - LA])
                if r is not None:
                    pending.append((idx + DEFER, r))
                if h + 1 < NH:
                    while nxt_done < 4 and CHK[nxt_done][1] <= sufmin[idx - LA + 1]:
                        kv_load(h + 1, nxt_done)
                        nxt_done += 1
            while pending and pending[0][0] <= idx:
                fin_part2(pending.pop(0)[1])
        if h + 1 < NH:
            while nxt_done < 4:
                kv_load(h + 1, nxt_done)
                nxt_done += 1
        while pending:
            fin_part2(pending.pop(0)[1])
    P.barrier()
    if stop_after == "E":
        P.finish()
        return nc

    mx = P.tile("mx", [128, KC, 1024], BF16)
    wo = [P.tile("wo", [128, KC, 512], BF16) for _ in range(2)]
    xr = [P.tile("xr", [128, 512], F32) for _ in range(2)]
    ho = [P.tile("ho", [128, 512], F32) for _ in range(2)]
    it = 0
    for half in range(2):
        P.emit("sync", lambda E, half=half: E.dma_start(
            out=mx.a[:], in_=mixT_d[:, :, 1024 * half:1024 * (half + 1)].rearrange("k p t -> p k t")),
            reads=[B_mixT], writes=[mx], dma=mx)
        for g in range(8):
            w_ = wo[(half * 8 + g) % 2]
            wload(w_, w_out, 512 * g, 512)
            for s in range(8):
                tok0 = 1024 * half + 128 * s
                pb = PS[it % 4]
                x_, o_ = xr[it % 2], ho[it % 2]
                it += 1
                P.emit("sync", lambda E, x_=x_, tok0=tok0, g=g: E.dma_start(
                    out=x_.a[:], in_=xw[WMAX + tok0:WMAX + tok0 + 128, 512 * g:512 * (g + 1)]), writes=[x_], dma=x_)
                for kc in range(KC):
                    P.emit("pe", lambda E, pb=pb, w_=w_, s=s, kc=kc: E.matmul(
                        pb.a[:], lhsT=mx.a[:, kc, s * 128:(s + 1) * 128], rhs=w_.a[:, kc, :], start=(kc == 0), stop=(kc == KC - 1)),
                        reads=[mx, w_], writes=[pb] if kc == 0 else [], pwrites=[] if kc == 0 else [pb])
                P.emit("dve", lambda E, pb=pb, x_=x_, o_=o_: E.tensor_tensor(out=o_.a[:], in0=pb.a[:], in1=x_.a[:], op=ALU.add),
                       reads=[pb, x_], writes=[o_])
                P.emit("sync", lambda E, o_=o_, tok0=tok0, g=g: E.dma_start(
                    out=h1_d[tok0:tok0 + 128, 512 * g:512 * (g + 1)], in_=o_.a[:]), reads=[o_], pwrites=[B_h1], dma=o_)
    P.barrier()
    norm_T([(h1_d, TQ // 128, hn2T_d, B_hn2T)], g_ffn, src_buf=B_h1)
    if stop_after == "F":
        P.finish()
        return nc

    for blk in range(4):
        hb2 = P.tile("hb2", [128, KC, 512], BF16)
        actT = P.tile("actT", [128, FC, 512], BF16)
        mark = P.off
        wg = [P.tile("wg", [128, KC, 256], BF16) for _ in range(2)]
        wu = [P.tile("wu", [128, KC, 256], BF16) for _ in range(2)]
        sgl = [P.tile("sgl", [128, 512], F32) for _ in range(2)]
        P.emit("sync", lambda E, blk=blk, hb2=hb2: E.dma_start(
            out=hb2.a[:], in_=hn2T_d[blk]),
            reads=[B_hn2T[blk]], writes=[hb2], dma=hb2)
        for g in range(DFF // 256):
            wg_, wu_ = wg[g % 2], wu[g % 2]
            wload(wg_, w_gate, 256 * g, 256)
            wload(wu_, w_up, 256 * g, 256)
            for c in range(2):
                fc = 2 * g + c
                pg, pu = PS[(fc % 2) * 2], PS[(fc % 2) * 2 + 1]
                for (pb, w_) in ((pg, wg_), (pu, wu_)):
                    for kc in range(KC):
                        P.emit("pe", lambda E, pb=pb, w_=w_, c=c, kc=kc, hb2=hb2: E.matmul(
                            pb.a[:], lhsT=w_.a[:, kc, c * 128:(c + 1) * 128], rhs=hb2.a[:, kc, :], start=(kc == 0), stop=(kc == KC - 1)),
                            reads=[w_, hb2], writes=[pb] if kc == 0 else [], pwrites=[] if kc == 0 else [pb])
                s_ = sgl[fc % 2]
                P.emit("act", lambda E, pg=pg, s_=s_: E.activation(out=s_.a[:], in_=pg.a[:], func=AF.Silu), reads=[pg], writes=[s_])
                P.emit("dve", lambda E, pu=pu, s_=s_, fc=fc, actT=actT: E.tensor_tensor(
                    out=actT.a[:, fc, :], in0=pu.a[:], in1=s_.a[:], op=ALU.mult), reads=[pu, s_], pwrites=[actT])
        P.barrier(keep=mark)
        wd = [P.tile("wd", [128, 8, 512], BF16) for _ in range(3)]
        hr = [P.tile("hr", [128, 512], F32) for _ in range(2)]
        ho2 = [P.tile("ho2", [128, 512], F32) for _ in range(2)]
        wdi = 0
        it = 0
        for g in range(8):
            for fb in range(0, FC, 8):
                nf = min(8, FC - fb)
                w_ = wd[wdi % 3]
                wdi += 1
                wload(w_, w_down, 512 * g, 512, r0=fb * 128, nk=nf)
                for k in range(nf):
                    fc = fb + k
                    for s in range(4):
                        pb = PS[(g % 2) * 4 + s]
                        P.emit("pe", lambda E, pb=pb, w_=w_, k=k, fc=fc, s=s, actT=actT: E.matmul(
                            pb.a[:], lhsT=actT.a[:, fc, s * 128:(s + 1) * 128], rhs=w_.a[:, k, :], start=(fc == 0), stop=(fc == FC - 1)),
                            reads=[actT, w_], writes=[pb] if fc == 0 else [], pwrites=[] if fc == 0 else [pb])
            for s in range(4):
                pb = PS[(g % 2) * 4 + s]
                tok0 = 512 * blk + 128 * s
                r_, o_ = hr[it % 2], ho2[it % 2]
                it += 1
                P.emit("sync", lambda E, r_=r_, tok0=tok0, g=g: E.dma_start(
                    out=r_.a[:], in_=h1_d[tok0:tok0 + 128, 512 * g:512 * (g + 1)]), reads=[B_h1], writes=[r_], dma=r_)
                P.emit("dve", lambda E, pb=pb, r_=r_, o_=o_: E.tensor_tensor(out=o_.a[:], in0=pb.a[:], in1=r_.a[:], op=ALU.add),
                       reads=[pb, r_], writes=[o_])
                P.emit("sync", lambda E, o_=o_, tok0=tok0, g=g: E.dma_start(
                    out=h2_d[tok0:tok0 + 128, 512 * g:512 * (g + 1)], in_=o_.a[:]), reads=[o_], pwrites=[B_h2], dma=o_)
        P.barrier()

    gb = P.tile("gbf", [128, D], F32)
    P.emit("sync", lambda E: E.dma_start(out=gb.a[:], in_=g_fin.partition_broadcast(128)), writes=[gb], dma=gb)
    xt = [P.tile("xtf", [128, D], F32) for _ in range(2)]
    sq = P.tile("sqf", [128, D], BF16)
    st = [P.tile("stf", [128, 2], F32) for _ in range(2)]
    ot = [P.tile("otf", [128, D], F32) for _ in range(2)]
    for i in range(TQ // 128):
        x_, s_, o_ = xt[i % 2], st[i % 2], ot[i % 2]
        P.emit("sync", lambda E, x_=x_, i=i: E.dma_start(out=x_.a[:], in_=h2_d[i * 128:(i + 1) * 128, :]),
               reads=[B_h2], writes=[x_], dma=x_)
        P.emit("act", lambda E, x_=x_, s_=s_: E.activation(out=sq.a[:], in_=x_.a[:], func=AF.Square, accum_out=s_.a[:, 0:1]),
               reads=[x_], writes=[sq], pwrites=[s_])
        P.emit("act", lambda E, s_=s_: E.activation(out=s_.a[:, 1:2], in_=s_.a[:, 0:1], func=AF.Ln, scale=1.0 / D, bias=EPS_COL),
               reads=[s_, cst_t], pwrites=[s_])
        P.emit("act", lambda E, s_=s_: E.activation(out=s_.a[:, 1:2], in_=s_.a[:, 1:2], func=AF.Exp, scale=-0.5),
               reads=[s_], pwrites=[s_])
        P.emit("dve", lambda E, x_=x_, s_=s_, o_=o_: E.scalar_tensor_tensor(
            out=o_.a[:], in0=x_.a[:], scalar=s_.a[:, 1:2], in1=gb.a[:], op0=ALU.mult, op1=ALU.mult),
            reads=[x_, s_, gb], writes=[o_])
        P.emit("sync", lambda E, o_=o_, i=i: E.dma_start(out=out_d[i * 128:(i + 1) * 128, :], in_=o_.a[:]),
               reads=[o_], pwrites=[B_out], dma=o_)
    P.barrier()
    P.finish()
    return nc


def _col(v, n):
    return np.ascontiguousarray(np.asarray(v, np.float32).reshape(n, 128).T)


def prepare_inputs(x, meta_tokens, norm_mix_g, w_in, lambda_q1, lambda_k1, lambda_q2, lambda_k2,
                   subln_g, b_glu, dw_kernel, dw_bias, conv_ln_g, conv_ln_b, w_out,
                   norm_ffn_g, w_gate, w_up, w_down, final_norm_g):
    f = lambda a: np.asarray(a, np.float32)
    xall = np.concatenate([f(meta_tokens), f(x)[0]], axis=0)
    x0 = f(x)[0]
    cst = np.zeros((128, 1024), np.float32)
    cst[:, 0:128] = np.eye(128, dtype=np.float32)
    cst[:, 128:256] = 1.0
    cst[:, 256:768] = np.arange(512, dtype=np.float32)[None, :] - np.arange(128, dtype=np.float32)[:, None]
    cst[:, 768] = f(lambda_q1)[0]
    cst[:, 769] = f(lambda_k1)[0]
    cst[:, 770] = f(lambda_q2)[0]
    cst[:, 771] = f(lambda_k2)[0]
    cst[:, 772] = math.log(1.0 - LAM_INIT)
    cst[:, 773] = EPS
    cst[:, 774] = LN_EPS
    qk = np.arange(512, dtype=np.float32)[None, :] - np.arange(128, dtype=np.float32)[:, None]
    adtab = np.stack([np.abs(qk - 128.0 * i) for i in range(4)], axis=1).astype(np.float32)
    lent, nl = local_entries()
    ltab = np.zeros((128, nl), np.float32)
    for lst in lent.values():
        for (t, mode, ca, cb, col) in lst:
            ltab[:, col] = cb
    dwk = np.ascontiguousarray(f(dw_kernel)[0].T.reshape(16, 128, 31).transpose(1, 0, 2).reshape(128, 16 * 31))
    shared = {
        "w_in": np.ascontiguousarray(f(w_in)[0]), "w_out": np.ascontiguousarray(f(w_out)[0]),
        "w_gate": np.ascontiguousarray(f(w_gate)[0]), "w_up": np.ascontiguousarray(f(w_up)[0]),
        "w_down": np.ascontiguousarray(f(w_down)[0]),
        "g_mix": np.ascontiguousarray(f(norm_mix_g)[0]), "g_ffn": np.ascontiguousarray(f(norm_ffn_g)[0]),
        "g_fin": np.ascontiguousarray(f(final_norm_g)), "g_sub": np.ascontiguousarray(f(subln_g)[0]),
        "cst": cst, "dwk": dwk, "adtab": np.ascontiguousarray(adtab), "ltab": ltab,
    }
    maps = []
    for c in range(NCORES):
        qlo = NMETA + TQ * c
        pos = qlo - WMAX + np.arange(NW)
        valid = (pos >= 0) & (pos < L)
        xw = np.zeros((NW, D), np.float32)
        xw[valid] = xall[pos[valid]]
        ptab = np.zeros((128, 384), np.float32)
        ptab[:, 0:32] = _col(f(b_glu)[0], 32)
        ptab[:, 32:48] = _col(f(dw_bias)[0], 16)
        ptab[:, 48:64] = _col(f(conv_ln_g)[0], 16)
        ptab[:, 64:80] = _col(f(conv_ln_b)[0], 16)
        o_mg = 80 + NW // 128
        o_em = o_mg + NG // 128
        ptab[:, 80:o_mg] = _col(valid.astype(np.float32), NW // 128)
        ptab[:, o_mg:o_em] = _col((np.arange(NG) < L).astype(np.float32), NG // 128)
        epos = qlo - 16 + np.concatenate([np.arange(16), NE - 16 + np.arange(16)])
        ptab[:, o_em:o_em + 32] = ((epos >= 0) & (epos < L)).astype(np.float32)[None, :]
        t = np.arange(NGT)[:, None]
        j = np.arange(4)[None, :]
        k0 = np.where(t < 128, NMETA + (128 * t + TQ * c) % SEQ, 0)
        dl = (qlo + 512 * j - k0).astype(np.float64)
        own = np.zeros_like(dl, dtype=bool)
        gtab = np.zeros((4, 128, 1032), np.float32)
        for hh in range(4):
            sl = SLOPES[4 + hh]
            a = np.where(dl > 0, -sl, sl)
            b = -sl * np.abs(dl)
            a = np.where(own, 0.0, a)
            b = np.where(own, -30000.0, b)
            gtab[hh, :, 0:516] = a.reshape(-1)[None, :]
            gtab[hh, :, 516:1032] = b.reshape(-1)[None, :]
        m = dict(shared)
        m["xw"] = xw
        xg = np.zeros((NG, D), np.float32)
        xg[:SEQ] = np.roll(x0, -TQ * c, axis=0)
        xg[SEQ:L] = f(meta_tokens)
        m["xg"] = xg
        m["ptab"] = ptab
        m["gtab"] = gtab
        maps.append(m)
    return maps


def kernel(**inputs):
    maps = prepare_inputs(**inputs)
    nc = build_program()
    res = run_bass_kernel_spmd(nc, maps, core_ids=list(range(NCORES)))
    outs = [np.asarray(res.results[c]["out"], dtype=np.float32) for c in range(NCORES)]
    return np.concatenate(outs, axis=0)[None]
```
